# Optimizing a Trainium2 kernel written in Bass

```python
import jax
import jax.numpy as jnp
from jax import lax
import numpy as np

D_MODEL = 1024
BATCH = 8
SEQ = 8192
DEPTH = 2

D_MIX = D_MODEL
HEAD_DIM = 64
ATTN_WIDTH = D_MIX // 4
ATTN_HEADS = ATTN_WIDTH // HEAD_DIM
DILATED_PATTERNS = ((128, 1), (512, 4), (2048, 16))
MASK_VALUE = -1e30
SSM_WIDTH = D_MIX // 2
SSM_HEADS = SSM_WIDTH // HEAD_DIM
SSM_GROUPS = 2
SSM_STATE = 128
SSM_CONV = 4
SSM_CHUNK = 128
SSM_BC = SSM_GROUPS * SSM_STATE
SSM_CONV_DIM = SSM_WIDTH + 2 * SSM_BC
HGRN_WIDTH = D_MIX // 4
HGRN_HEADS = HGRN_WIDTH // HEAD_DIM
HGRN_EXPAND = 64
HGRN_KEY_WIDTH = HGRN_HEADS * HGRN_EXPAND
HGRN_CHUNK = 64
HGRN_LB_FLOOR = 1e-20
D_FF = 256 * ((8 * D_MODEL // 3 + 255) // 256)
NORM_EPS = 1e-6
IN_SIZES = (ATTN_WIDTH, ATTN_WIDTH, ATTN_WIDTH,
            SSM_WIDTH, SSM_CONV_DIM, SSM_HEADS,
            HGRN_KEY_WIDTH, HGRN_KEY_WIDTH, HGRN_WIDTH, HGRN_WIDTH)
D_IN_PROJ = sum(IN_SIZES)

kernel_name = 'hybrid_dilated_ssd_hgrn2_macaron'


def _rms(x):
    xf = x.astype(jnp.float32)
    return xf * lax.rsqrt(jnp.mean(xf * xf, axis=-1, keepdims=True) + NORM_EPS)


def rms_norm(x, w):
    return (_rms(x) * w.astype(jnp.float32)).astype(x.dtype)


def swiglu(x, w_gate, w_up, w_down):
    return (jax.nn.silu(x @ w_gate) * (x @ w_up)) @ w_down


def _split_columns(p):
    out, start = [], 0
    for size in IN_SIZES:
        out.append(p[..., start:start + size])
        start += size
    return out


def _masked_exp(mask, logits):
    return jnp.where(mask, jnp.exp(jnp.where(mask, logits, 0.0)), 0.0)


def _dilated_window_branch(q, k, v, window, dilation):
    b, L, h, hd = q.shape
    blk = window // dilation
    span = blk * dilation
    Lp = -(-L // span) * span
    M = Lp // dilation
    nb = M // blk

    def split(t):
        t = jnp.pad(t, ((0, 0), (0, Lp - L), (0, 0), (0, 0))).reshape(b, M, dilation, h, hd)
        return t.transpose(0, 2, 3, 1, 4).reshape(b, dilation, h, nb, blk, hd)

    def with_prev(t):
        prev = jnp.pad(t, ((0, 0), (0, 0), (0, 0), (1, 0), (0, 0), (0, 0)))[:, :, :, :-1]
        return jnp.concatenate([prev, t], axis=4)

    qb = split(q).astype(jnp.float32)
    kk = with_prev(split(k)).astype(jnp.float32)
    vv = with_prev(split(v)).astype(jnp.float32)
    s = jnp.einsum('brhnqd,brhnkd->brhnqk', qb, kk) * (hd ** -0.5)
    qi = jnp.arange(blk)[:, None]
    ki = jnp.arange(2 * blk)[None, :]
    band = (ki >= qi) & (ki <= qi + blk)
    not_before_start = (jnp.arange(nb)[:, None, None] > 0) | (ki[None] >= blk)
    valid = band[None] & not_before_start
    s = jnp.where(valid, s, MASK_VALUE)
    m = jnp.max(s, axis=-1)
    p = jnp.where(valid, jnp.exp(s - m[..., None]), 0.0)
    l = jnp.sum(p, axis=-1)
    o = jnp.einsum('brhnqk,brhnkd->brhnqd', p, vv) / l[..., None]

    def merge(t):
        t = t.reshape((b, dilation, h, M) + t.shape[5:])
        t = jnp.moveaxis(t, (1, 2), (2, 3))
        return t.reshape((b, Lp, h) + t.shape[4:])[:, :L]

    return merge(o), merge(m), merge(l)


def dilated_attention(q, k, v):
    outs = [_dilated_window_branch(q, k, v, w, d) for (w, d) in DILATED_PATTERNS]
    m_all = jnp.stack([m for (_, m, _) in outs])
    l_all = jnp.stack([l for (_, _, l) in outs])
    o_all = jnp.stack([o for (o, _, _) in outs])
    wts = l_all * jnp.exp(m_all - jnp.max(m_all, axis=0, keepdims=True))
    o = jnp.sum(wts[..., None] * o_all, axis=0) / jnp.sum(wts, axis=0)[..., None]
    return o.astype(q.dtype)


def causal_depthwise_conv(x, w, bias):
    k_width, ch = w.shape
    y = lax.conv_general_dilated(x, w[:, None, :].astype(x.dtype), window_strides=(1,),
                                 padding=((k_width - 1, 0),),
                                 dimension_numbers=('NWC', 'WIO', 'NWC'),
                                 feature_group_count=ch)
    return y + bias.astype(x.dtype)


def ssd_chunked(x, a, Bm, Cm, chunk):
    b, L, g, e, p = x.shape
    n = Bm.shape[-1]
    nc = L // chunk
    xc = x.reshape(b, nc, chunk, g, e, p)
    Bc = Bm.reshape(b, nc, chunk, g, n)
    Cc = Cm.reshape(b, nc, chunk, g, n)
    a_cs = jnp.cumsum(a.reshape(b, nc, chunk, g, e).transpose(0, 1, 3, 4, 2), axis=-1)
    causal = jnp.tril(jnp.ones((chunk, chunk), dtype=bool))
    seg = a_cs[..., :, None] - a_cs[..., None, :]
    Lmat = _masked_exp(causal, seg)
    cb = jnp.einsum('bctgn,bcsgn->bcgts', Cc, Bc)
    y_diag = jnp.einsum('bcgts,bcgets,bcsgep->bctgep', cb, Lmat, xc)
    decay_states = jnp.exp(a_cs[..., -1:] - a_cs)
    states = jnp.einsum('bcsgn,bcges,bcsgep->bcgepn', Bc, decay_states, xc)
    chunk_decay = jnp.exp(a_cs[..., -1])

    def step(hstate, inp):
        st, dec = inp
        return dec[..., None, None] * hstate + st, hstate

    h0 = jnp.zeros((b, g, e, p, n), dtype=x.dtype)
    _, prev = lax.scan(step, h0, (jnp.moveaxis(states, 1, 0), jnp.moveaxis(chunk_decay, 1, 0)))
    prev = jnp.moveaxis(prev, 0, 1)
    y_off = jnp.einsum('bctgn,bcgepn,bcget->bctgep', Cc, prev, jnp.exp(a_cs))
    return (y_diag + y_off).reshape(b, L, g, e, p)


def mamba2_mixer(z, xbc, dt_raw, conv_w, conv_b, dt_bias, a_log, d_skip, norm_w):
    b, L, _ = xbc.shape
    e = SSM_HEADS // SSM_GROUPS
    hp = SSM_WIDTH // SSM_HEADS
    xbc = jax.nn.silu(causal_depthwise_conv(xbc, conv_w, conv_b)).astype(jnp.float32)
    xs = xbc[..., :SSM_WIDTH].reshape(b, L, SSM_GROUPS, e, hp)
    Bm = xbc[..., SSM_WIDTH:SSM_WIDTH + SSM_BC].reshape(b, L, SSM_GROUPS, SSM_STATE)
    Cm = xbc[..., SSM_WIDTH + SSM_BC:].reshape(b, L, SSM_GROUPS, SSM_STATE)
    dt = jax.nn.softplus(dt_raw.astype(jnp.float32) + dt_bias.astype(jnp.float32))
    dt = dt.reshape(b, L, SSM_GROUPS, e)
    A = -jnp.exp(a_log.astype(jnp.float32)).reshape(SSM_GROUPS, e)
    y = ssd_chunked(xs * dt[..., None], A * dt, Bm, Cm, SSM_CHUNK)
    y = y + d_skip.astype(jnp.float32).reshape(SSM_GROUPS, e)[:, :, None] * xs
    y = y.reshape(b, L, SSM_WIDTH) * jax.nn.silu(z.astype(jnp.float32))
    y = _rms(y.reshape(b, L, SSM_GROUPS, SSM_WIDTH // SSM_GROUPS)).reshape(b, L, SSM_WIDTH)
    return (y * norm_w.astype(jnp.float32)).astype(z.dtype)


def chunk_gated_recurrence(q, k, v, log_f, chunk):
    b, L, H, dk = q.shape
    dv = v.shape[-1]
    nc = L // chunk

    def to_chunks(t):
        return t.reshape(b, nc, chunk, H, t.shape[-1]).transpose(1, 0, 3, 2, 4)

    causal = jnp.tril(jnp.ones((chunk, chunk), dtype=bool))[:, :, None]

    def step(S, inp):
        qi, ki, vi, fi = inp
        bcum = jnp.cumsum(fi, axis=2)
        o_inter = jnp.einsum('bhtk,bhkv->bhtv', qi * jnp.exp(bcum), S)
        rel = bcum[:, :, :, None, :] - bcum[:, :, None, :, :]
        decay = _masked_exp(causal, rel)
        att = jnp.einsum('bhtk,bhtsk,bhsk->bhts', qi, decay, ki)
        o_intra = jnp.einsum('bhts,bhsv->bhtv', att, vi)
        blast = bcum[:, :, -1]
        k_dec = ki * jnp.exp(blast[:, :, None] - bcum)
        S = jnp.exp(blast)[..., None] * S + jnp.einsum('bhsk,bhsv->bhkv', k_dec, vi)
        return S, o_inter + o_intra

    S0 = jnp.zeros((b, H, dk, dv), dtype=q.dtype)
    _, o = lax.scan(step, S0, (to_chunks(q), to_chunks(k), to_chunks(v), to_chunks(log_f)))
    return o.transpose(1, 0, 3, 2, 4).reshape(b, L, H, dv)


def hgrn2_lower_bounds(lb_logits):
    sm = jax.nn.softmax(lb_logits.astype(jnp.float32), axis=0)
    return jnp.cumsum(sm, axis=0) - sm[0]


def hgrn2_mixer(hq, hf, hi, hg, lb, norm_w):
    b, L, _ = hq.shape
    dv = HGRN_WIDTH // HGRN_HEADS
    q = jax.nn.silu(hq.astype(jnp.float32)).reshape(b, L, HGRN_HEADS, HGRN_EXPAND)
    lb = jnp.clip(lb, HGRN_LB_FLOOR, 1.0 - 1e-6)
    log_f = jnp.logaddexp(jnp.log(lb), jnp.log1p(-lb) + jax.nn.log_sigmoid(hf.astype(jnp.float32)))
    log_f = log_f.reshape(b, L, HGRN_HEADS, HGRN_EXPAND)
    k = -jnp.expm1(log_f)
    v = hi.astype(jnp.float32).reshape(b, L, HGRN_HEADS, dv)
    o = chunk_gated_recurrence(q, k, v, log_f, HGRN_CHUNK)
    o = _rms(o).reshape(b, L, HGRN_WIDTH) * norm_w.astype(jnp.float32)
    return (o * jax.nn.silu(hg.astype(jnp.float32))).astype(hq.dtype)


def setup_inputs(seed: int = 0) -> dict:
    key = jax.random.key(seed)
    ks = jax.random.split(key, 24)

    def nrm(k, shape, scale):
        return jax.random.normal(k, shape, jnp.float32) * scale

    def gain(k, shape):
        return 1.0 + 0.05 * jax.random.normal(k, shape, jnp.float32)

    dt0 = jnp.exp(jax.random.uniform(ks[9], (DEPTH, SSM_HEADS), jnp.float32,
                                     np.log(1e-3).astype(np.float32), np.log(1e-1).astype(np.float32)))
    return {
        'x': nrm(ks[0], (BATCH, SEQ, D_MODEL), 1.0),
        'ffn1_norm': gain(ks[1], (DEPTH, D_MODEL)),
        'ffn1_w_gate': nrm(ks[2], (DEPTH, D_MODEL, D_FF), D_MODEL ** -0.5),
        'ffn1_w_up': nrm(ks[3], (DEPTH, D_MODEL, D_FF), D_MODEL ** -0.5),
        'ffn1_w_down': nrm(ks[4], (DEPTH, D_FF, D_MODEL), D_FF ** -0.5),
        'mix_norm': gain(ks[5], (DEPTH, D_MODEL)),
        'w_in': nrm(ks[6], (DEPTH, D_MODEL, D_IN_PROJ), D_MODEL ** -0.5),
        'conv_w': nrm(ks[7], (DEPTH, SSM_CONV, SSM_CONV_DIM), SSM_CONV ** -0.5),
        'conv_b': nrm(ks[8], (DEPTH, SSM_CONV_DIM), 0.02),
        'dt_bias': dt0 + jnp.log(-jnp.expm1(-dt0)),
        'a_log': jnp.log(jax.random.uniform(ks[10], (DEPTH, SSM_HEADS), jnp.float32, 1.0, 16.0)),
        'd_skip': gain(ks[11], (DEPTH, SSM_HEADS)),
        'ssm_norm': gain(ks[12], (DEPTH, SSM_WIDTH)),
        'hgrn_lb_logits': nrm(ks[13], (DEPTH, HGRN_KEY_WIDTH), 0.1),
        'hgrn_norm': gain(ks[14], (DEPTH, HGRN_WIDTH)),
        'w_out': nrm(ks[15], (DEPTH, D_MIX, D_MODEL), D_MIX ** -0.5),
        'ffn2_norm': gain(ks[16], (DEPTH, D_MODEL)),
        'ffn2_w_gate': nrm(ks[17], (DEPTH, D_MODEL, D_FF), D_MODEL ** -0.5),
        'ffn2_w_up': nrm(ks[18], (DEPTH, D_MODEL, D_FF), D_MODEL ** -0.5),
        'ffn2_w_down': nrm(ks[19], (DEPTH, D_FF, D_MODEL), D_FF ** -0.5),
        'final_norm': gain(ks[20], (D_MODEL,)),
    }


def reference(x, ffn1_norm, ffn1_w_gate, ffn1_w_up, ffn1_w_down, mix_norm, w_in, conv_w, conv_b,
              dt_bias, a_log, d_skip, ssm_norm, hgrn_lb_logits, hgrn_norm, w_out, ffn2_norm,
              ffn2_w_gate, ffn2_w_up, ffn2_w_down, final_norm):
    b, L, _ = x.shape
    lower_bounds = hgrn2_lower_bounds(hgrn_lb_logits)
    h = x
    for layer in range(DEPTH):
        h = h + 0.5 * swiglu(rms_norm(h, ffn1_norm[layer]), ffn1_w_gate[layer],
                             ffn1_w_up[layer], ffn1_w_down[layer])
        u = rms_norm(h, mix_norm[layer])
        aq, ak, av, z, xbc, dt_raw, hq, hf, hi, hg = _split_columns(u @ w_in[layer])
        heads = lambda t: t.reshape(b, L, ATTN_HEADS, HEAD_DIM)
        y_attn = dilated_attention(heads(aq), heads(ak), heads(av)).reshape(b, L, ATTN_WIDTH)
        y_ssm = mamba2_mixer(z, xbc, dt_raw, conv_w[layer], conv_b[layer], dt_bias[layer],
                             a_log[layer], d_skip[layer], ssm_norm[layer])
        y_hgrn = hgrn2_mixer(hq, hf, hi, hg, lower_bounds[layer], hgrn_norm[layer])
        y = jnp.concatenate([y_attn.astype(h.dtype), y_ssm.astype(h.dtype), y_hgrn.astype(h.dtype)], axis=-1)
        h = h + y @ w_out[layer]
        h = h + 0.5 * swiglu(rms_norm(h, ffn2_norm[layer]), ffn2_w_gate[layer],
                             ffn2_w_up[layer], ffn2_w_down[layer])
    return rms_norm(h, final_norm)
```

```python
import numpy as np
from contextlib import ExitStack
import concourse.bass as bass
import concourse.mybir as mybir
from concourse.bass_utils import run_bass_kernel_spmd

F32 = mybir.dt.float32
BF16 = mybir.dt.bfloat16
ALU = mybir.AluOpType
AF = mybir.ActivationFunctionType

D = 1024
DFF = 2816
NFC = DFF // 128
DEPTH = 2
DIN = 3336
EPS = 1e-6


class TL:
    def __init__(self, name, sem, step):
        self.name, self.sem, self.step, self.val = name, sem, step, 0


class Res:
    __slots__ = ("name", "w", "r", "excl")

    def __init__(self, name, excl=False):
        self.name = name
        self.w = None
        self.r = {}
        self.excl = excl


class KB:
    def __init__(self, nc, es):
        self.nc = nc
        self.es = es
        self.eng = {"pe": nc.tensor, "act": nc.scalar, "dve": nc.vector, "pool": nc.gpsimd, "sp": nc.sync}
        self.tl = {}
        for e in ("pe", "act", "dve", "pool"):
            self.tl[e] = TL(e, es.enter_context(nc.semaphore("sem_" + e)), 1)
        self.known = {e: {} for e in self.eng}
        self.all_tls = list(self.tl.values())
        self.nres = 0
        self.ninst = {e: 0 for e in self.eng}
        self.needed = {e: set() for e in self.tl}
        self.needed_in = None
        self.phys = {e: 0 for e in self.tl}
        self.l2p = {e: {} for e in self.tl}
        self.tl_eng = {id(t): e for e, t in self.tl.items()}

    def res(self, name=None, excl=False):
        self.nres += 1
        return Res(name or f"r{self.nres}", excl)

    def dma_tl(self, name):
        t = TL(name, self.es.enter_context(self.nc.semaphore("dsem_" + name)), 16)
        self.all_tls.append(t)
        return t

    def _wait(self, eng, deps):
        kn = self.known[eng]
        own = self.tl.get(eng)
        need = {}
        for (tl, val) in deps:
            if tl is own:
                if eng == "pe":
                    continue
                if own.val - val >= 3:
                    continue
            if kn.get(tl, 0) >= val:
                continue
            if need.get(tl, 0) < val:
                need[tl] = val
        for tl, val in need.items():
            self._emit_wait(eng, tl, val)
            kn[tl] = val

    def _emit_wait(self, eng, tl, val):
        te = self.tl_eng.get(id(tl))
        pv = val
        if te is not None:
            self.needed[te].add(val)
            if self.needed_in is not None:
                pv = self.l2p[te][val]
        self.eng[eng].wait_ge(tl.sem, pv)
        self.ninst[eng] += 1

    def _deps(self, reads, writes):
        deps = []
        for r in reads:
            if r.w is not None:
                deps.append(r.w)
            if r.excl:
                deps.extend(r.r.items())
        for w in writes:
            if w.w is not None:
                deps.append(w.w)
            deps.extend(w.r.items())
        return deps

    def _stamp(self, tl, reads, writes):
        for r in reads:
            if r.r.get(tl, 0) < tl.val:
                r.r[tl] = tl.val
        for w in writes:
            w.w = (tl, tl.val)
            w.r = {}

    def op(self, eng, fn, reads=(), writes=()):
        self._wait(eng, self._deps(reads, writes))
        inst = fn(self.eng[eng])
        tl = self.tl[eng]
        tl.val += 1
        if self.needed_in is None or tl.val in self.needed_in[eng]:
            inst.then_inc(tl.sem, 1)
            self.phys[eng] += 1
            self.l2p[eng][tl.val] = self.phys[eng]
        self.ninst[eng] += 1
        self._stamp(tl, reads, writes)
        return inst

    def dma(self, eng, tl, out, in_, reads=(), writes=(), **kw):
        deps = self._deps(reads, writes)
        if tl.val > 0:
            deps.append((tl, tl.val))
        self._wait(eng, deps)
        inst = self.eng[eng].dma_start(out=out, in_=in_, **kw)
        tl.val += 16
        inst.then_inc(tl.sem, 16)
        self.ninst[eng] += 1
        self._stamp(tl, reads, writes)
        return inst

    def barrier(self, engs=("pe", "act", "dve", "pool", "sp")):
        for e in engs:
            self._wait(e, [(t, t.val) for t in self.all_tls if t.val > 0 and t is not self.tl.get(e)])
            own = self.tl.get(e)
            if own is not None and own.val > 0 and e != "pe":
                if self.known[e].get(own, 0) < own.val:
                    self._emit_wait(e, own, own.val)
                    self.known[e][own] = own.val


class Prog:
    def __init__(self, L, n_layers=DEPTH, do_mixer=True, mix_parts=("attn", "ssd", "hgrn"), needed_in=None):
        self.L = L
        self.n_layers = n_layers
        self.do_mixer = do_mixer
        self.mix_parts = mix_parts
        self.nc = bass.Bass("TRN2", target_bir_lowering=False)
        self.es = ExitStack()
        self.k = KB(self.nc, self.es)
        self.k.needed_in = needed_in

    def sb(self, es, name, shape, dt):
        self._uid = getattr(self, "_uid", 0) + 1
        return es.enter_context(self.nc.sbuf_tensor(f"{name}_{self._uid}", shape, dt))

    def pst(self, es, name, shape, dt):
        self._uid = getattr(self, "_uid", 0) + 1
        return es.enter_context(self.nc.psum_tensor(f"{name}_{self._uid}", shape, dt))

    def declare(self):
        nc, L = self.nc, self.L
        di = lambda n, s: nc.dram_tensor(n, s, F32, kind="ExternalInput").ap()
        self.x = di("x", [L, D])
        self.inp = {}
        shapes = {
            "ffn1_norm": [DEPTH, D], "ffn1_w_gate": [DEPTH, D, DFF], "ffn1_w_up": [DEPTH, D, DFF],
            "ffn1_w_down": [DEPTH, DFF, D], "mix_norm": [DEPTH, D], "w_in": [DEPTH, D, DIN],
            "conv_w": [DEPTH, 4, 1024], "conv_b": [DEPTH, 1024], "dt_bias": [DEPTH, 8], "a_log": [DEPTH, 8],
            "d_skip": [DEPTH, 8], "ssm_norm": [DEPTH, 512], "hgrn_lb_logits": [DEPTH, 256],
            "hgrn_norm": [DEPTH, 256], "w_out": [DEPTH, D, D], "ffn2_norm": [DEPTH, D],
            "ffn2_w_gate": [DEPTH, D, DFF], "ffn2_w_up": [DEPTH, D, DFF], "ffn2_w_down": [DEPTH, DFF, D],
            "final_norm": [D],
        }
        for n, s in shapes.items():
            self.inp[n] = di(n, s)
        self.out = nc.dram_tensor("out", [L, D], F32, kind="ExternalOutput").ap()
        self.hbuf = nc.dram_tensor("hbuf", [128, 8, L], F32, kind="Internal").ap()
        self.win = [nc.dram_tensor(f"win_{l}", [128, 8, DIN], BF16, kind="Internal").ap() for l in range(self.n_layers)]
        self.wout = [nc.dram_tensor(f"wout_{l}", [128, 8, D], BF16, kind="Internal").ap() for l in range(self.n_layers)]
        self.wgu = {}
        self.wd = {}
        for l in range(self.n_layers):
            for f in (1, 2):
                self.wgu[(l, f)] = nc.dram_tensor(f"wgu_{l}_{f}", [11, 128, 2, 8, 256], BF16, kind="Internal").ap()
                self.wd[(l, f)] = nc.dram_tensor(f"wd_{l}_{f}", [8, 128, NFC, 128], BF16, kind="Internal").ap()

    def convert_ffn(self, l, f):
        k = self.k
        wg = self.inp[f"ffn{f}_w_gate"][l].rearrange("(kc p) c -> p kc c", p=128)
        wu = self.inp[f"ffn{f}_w_up"][l].rearrange("(kc p) c -> p kc c", p=128)
        wdn = self.inp[f"ffn{f}_w_down"][l].rearrange("(fc p) c -> p fc c", p=128)
        self.cv_g = getattr(self, "cv_g", {})
        self.cv_d = getattr(self, "cv_d", {})
        for gi, grp in enumerate(((0, 1), (2, 3, 4), (5, 6, 7), (8, 9, 10))):
            tl = k.dma_tl(f"cvg{l}{f}{gi}")
            r = k.res(f"cvg{l}{f}{gi}")
            for g in grp:
                self._cv(tl, r, self.wgu[(l, f)][g, :, 0, :, :], wg[:, :, g * 256:(g + 1) * 256])
                self._cv(tl, r, self.wgu[(l, f)][g, :, 1, :, :], wu[:, :, g * 256:(g + 1) * 256])
                self.cv_g[(l, f, g)] = r
        for gi, grp in enumerate(((0, 1, 2, 3), (4, 5, 6, 7))):
            tl = k.dma_tl(f"cvd{l}{f}{gi}")
            r = k.res(f"cvd{l}{f}{gi}")
            for dc in grp:
                self._cv(tl, r, self.wd[(l, f)][dc, :, :, :], wdn[:, :, dc * 128:(dc + 1) * 128])
                self.cv_d[(l, f, dc)] = r

    def convert_mix(self, l):
        k = self.k
        if not hasattr(self, "mixcv_res"):
            self.mixcv_res = {}
        tl = k.dma_tl(f"cvm{l}")
        r = k.res(f"cvmres{l}")
        self.mixcv_res[l] = r
        wi = self.inp["w_in"][l].rearrange("(kc p) c -> p kc c", p=128)
        wo = self.inp["w_out"][l].rearrange("(kc p) c -> p kc c", p=128)
        for kc in range(8):
            self._cv(tl, r, self.win[l][:, kc, :], wi[:, kc, :])
            self._cv(tl, r, self.wout[l][:, kc, :], wo[:, kc, :])

    def convert_layer(self, l):
        self.convert_ffn(l, 1)
        if self.do_mixer:
            self.convert_mix(l)
        self.convert_ffn(l, 2)

    def _cv(self, tl, r, out, in_):
        k = self.k
        inst = k.eng["pool"].dma_start(out=out, in_=in_)
        tl.val += 16
        inst.then_inc(tl.sem, 16)
        r.w = (tl, tl.val)

    def setup_consts(self):
        k, nc, es = self.k, self.nc, self.es
        self.ones_b = self.sb(es, "ones_b", [128, 128], BF16)
        self.r_const = k.res("consts")
        self.cst_d = nc.dram_tensor("cst_d", [128, 128 + 1024 + 128 + 512], F32, kind="ExternalInput").ap()
        self.pk_d = nc.dram_tensor("pk_d", [128, DEPTH * PK_L], F32, kind="ExternalInput").ap()
        self.pk = self.sb(es, "pk", [128, DEPTH * PK_L], F32)
        self.tri_b = self.sb(es, "tri_b", [128, 128], BF16)
        tl = k.dma_tl("const")
        self.const_tl = tl
        self.cst_f = self.sb(es, "cst_f", [128, 128 + 1024 + 128 + 512], F32)
        self.rst_f = self.cst_f[:, 1280:1792]
        self.blk_b = self.sb(es, "blk_b", [128, 128], BF16)
        self.ident_f = self.cst_f[:, 0:128]
        self.tri_f = self.cst_f[:, 256:384]
        self.mask_b = self.sb(es, "mask_b", [128, 1024], BF16)
        self.ones_f = self.sb(es, "ones_f", [128, 128], F32)
        self.ident_b = self.sb(es, "ident_b", [128, 128], BF16)
        k.dma("sp", tl, self.cst_f[:], self.cst_d[:, :], writes=[self.r_const])
        k.dma("sp", tl, self.pk[:], self.pk_d[:, :], writes=[self.r_const])
        k.op("dve", lambda e: e.tensor_copy(out=self.blk_b[:], in_=self.cst_f[:, 1152:1280]),
             reads=[self.r_const], writes=[self.r_const])
        k.op("dve", lambda e: e.tensor_copy(out=self.tri_b[:], in_=self.cst_f[:, 256:384]),
             reads=[self.r_const], writes=[self.r_const])
        k.op("dve", lambda e: e.memset(self.ones_b[:], 1.0), writes=[self.r_const])
        k.op("dve", lambda e: e.memset(self.ones_f[:], 1.0), writes=[self.r_const])
        k.op("dve", lambda e: e.tensor_copy(out=self.mask_b[:], in_=self.cst_f[:, 128:1152]),
             reads=[self.r_const], writes=[self.r_const])
        self.mbias_b = self.sb(es, "mbias_b", [128, 1024], BF16)
        k.op("dve", lambda e: e.tensor_scalar(out=self.mbias_b[:], in0=self.cst_f[:, 128:1152], scalar1=-1.0, scalar2=30000.0,
                                              op0=ALU.add, op1=ALU.mult), reads=[self.r_const], writes=[self.r_const])
        k.op("dve", lambda e: e.tensor_copy(out=self.ident_b[:], in_=self.cst_f[:, 0:128]),
             reads=[self.r_const], writes=[self.r_const])
        self.normw = {}
        names = []
        for l in range(self.n_layers):
            names += [("ffn1_norm", l), ("mix_norm", l), ("ffn2_norm", l)]
        names.append(("final_norm", None))
        self.normw_sb = self.sb(es, "normw", [128, len(names), 8], F32)
        for i, (n, l) in enumerate(names):
            src = self.inp[n][l] if l is not None else self.inp[n]
            k.dma("sp", tl, self.normw_sb[:, i, :], src.rearrange("(c p) -> p c", p=128),
                  writes=[self.r_const], allow_slow_non_contiguous=True)
            self.normw[(n, l)] = i
        k.op("dve", lambda e: e.tensor_scalar(out=self.normw_sb[:], in0=self.normw_sb[:], scalar1=float(np.sqrt(D)),
                                              scalar2=None, op0=ALU.mult),
             reads=[self.r_const], writes=[self.r_const])

    def rmsnorm(self, h_ap, r_h, nw_idx, out_fn, r_out, sq, r_sq, ps, r_ps, rstd, r_rstd, ntok):
        k = self.k
        k.op("act", lambda e: e.activation(out=sq[:, :, :ntok], in_=h_ap, func=AF.Square),
             reads=[r_h], writes=[r_sq])
        for c in range(8):
            k.op("pe", lambda e, c=c: e.matmul(ps[:, :ntok], lhsT=self.ones_b[:, :], rhs=sq[:, c, :ntok],
                                               start=(c == 0), stop=(c == 7)),
                 reads=[r_sq, self.r_const], writes=[r_ps])
        k.op("act", lambda e: e.activation(out=rstd[:, :ntok], in_=ps[:, :ntok], func=AF.Sqrt,
                                           bias=float(D * EPS), scale=1.0),
             reads=[r_ps], writes=[r_rstd])
        k.op("dve", lambda e: e.reciprocal(out=rstd[:, :ntok], in_=rstd[:, :ntok]),
             reads=[r_rstd], writes=[r_rstd])
        for c in range(8):
            k.op("dve", lambda e, c=c: e.scalar_tensor_tensor(
                out=out_fn(c), in0=h_ap[:, c, :], scalar=self.normw_sb[:, nw_idx, c:c + 1], in1=rstd[:, :ntok],
                op0=ALU.mult, op1=ALU.mult),
                 reads=[r_h, r_rstd, self.r_const], writes=[r_out])

    def ffn_phase(self, l, f, first, last):
        k, nc, L = self.k, self.nc, self.L
        TF = 1024 if L >= 1024 else L
        NS = TF // 512
        ntiles = L // TF
        with ExitStack() as es:
            h_sb = [self.sb(es, f"h{i}", [128, 8, TF], F32) for i in range(2)]
            r_h = [k.res(f"h{i}") for i in range(2)]
            tl_h = [k.dma_tl(f"h{l}{f}{i}") for i in range(2)]
            tl_st = [k.dma_tl(f"st{l}{f}{i}") for i in range(2)]
            xn2 = [self.sb(es, f"xn{i}", [128, 8, TF], BF16) for i in range(2)]
            r_xn2 = [[k.res(f"xn{i}_{s}") for s in range(NS)] for i in range(2)]
            act = self.sb(es, "act", [128, NFC, TF], BF16)
            r_act = [[k.res() for s in range(NS)] for c in range(NFC)]
            NW = 2
            wgu = [self.sb(es, f"wgu{i}", [128, 2, 8, 256], BF16) for i in range(NW)]
            r_wgu = [k.res(f"wgu{i}") for i in range(NW)]
            tl_wgu = [k.dma_tl(f"wgu{l}{f}{i}") for i in range(NW)]
            wd = [self.sb(es, f"wd{i}", [128, NFC, 128], BF16) for i in range(2)]
            r_wd = [k.res(f"wd{i}") for i in range(2)]
            tl_wd = [k.dma_tl(f"wd{l}{f}{i}") for i in range(2)]
            sq = self.sb(es, "sq", [128, 8, 512], BF16)
            r_sq = k.res("sq")
            rstd = self.sb(es, "rstd", [128, 512], F32)
            r_rstd = k.res("rstd")
            sg = [self.sb(es, f"sg{i}", [128, 512], F32) for i in range(2)]
            r_sg = [k.res() for i in range(2)]
            if first:
                xtok = [self.sb(es, f"xtok{i}", [128, D], F32) for i in range(2)]
                r_xtok = [k.res() for i in range(2)]
                tl_xtok = [k.dma_tl(f"xtok{i}") for i in range(2)]
            if last:
                otok = [self.sb(es, f"otok{i}", [128, D], F32) for i in range(2)]
                r_otok = [k.res() for i in range(2)]
                tl_otok = [k.dma_tl(f"otok{i}") for i in range(2)]
            ps = [self.pst(es, f"ps{i}", [128, 512], F32) for i in range(8)]
            r_ps = [k.res(f"ps{i}", excl=True) for i in range(8)]
            nwi = self.normw[(f"ffn{f}_norm", l)]
            wgu_d, wd_d = self.wgu[(l, f)], self.wd[(l, f)]
            r_hb = self.r_hbuf

            wq = {"g": 0, "d": 0}

            def load_tile(t):
                b = t % 2
                if not first:
                    k.dma("sp", tl_h[b], h_sb[b][:], self.hbuf[:, :, t * TF:(t + 1) * TF],
                          reads=[r_hb[t]], writes=[r_h[b]])
                else:
                    for blk in range(TF // 128):
                        xb = blk % 2
                        t0 = t * TF + blk * 128
                        k.dma("sp", tl_xtok[xb], xtok[xb][:], self.x[t0:t0 + 128, :], writes=[r_xtok[xb]])
                        for half in range(2):
                            pi = 0 if half == 0 else 7
                            for c4 in range(4):
                                c = half * 4 + c4
                                k.op("pe", lambda e, c=c, c4=c4, pi=pi, xb=xb: e.transpose(
                                    out=ps[pi][:, c4 * 128:(c4 + 1) * 128], in_=xtok[xb][:, c * 128:(c + 1) * 128],
                                    identity=self.ident_f),
                                     reads=[r_xtok[xb], self.r_const], writes=[r_ps[pi]])
                            k.op("act" if half == 0 else "dve", lambda e, half=half, pi=pi, blk=blk, b=b: (
                                e.activation(out=h_sb[b][:, half * 4:(half + 1) * 4, blk * 128:(blk + 1) * 128],
                                             in_=ps[pi][:, :].rearrange("p (c t) -> p c t", c=4), func=AF.Copy)
                                if half == 0 else
                                e.tensor_copy(out=h_sb[b][:, half * 4:(half + 1) * 4, blk * 128:(blk + 1) * 128],
                                              in_=ps[pi][:, :].rearrange("p (c t) -> p c t", c=4))),
                                 reads=[r_ps[pi]], writes=[r_h[b]])

            def norm_tile(t):
                b = t % 2
                for s in range(NS):
                    sl = slice(s * 512, (s + 1) * 512)
                    self.rmsnorm(h_sb[b][:, :, sl], r_h[b], nwi, lambda c, sl=sl, b=b: xn2[b][:, c, sl], r_xn2[b][s],
                                 sq, r_sq, ps[0], r_ps[0], rstd, r_rstd, 512)

            load_tile(0)
            norm_tile(0)
            for t in range(ntiles):
                b = t % 2
                xn, r_xn = xn2[b], r_xn2[b]
                if t + 1 < ntiles and not first:
                    load_tile(t + 1)
                for g in range(11):
                    wslot = wq["g"] % NW
                    wq["g"] += 1
                    k.dma("sp", tl_wgu[wslot], wgu[wslot][:], wgu_d[g], reads=[self.cv_g[(l, f, g)]], writes=[r_wgu[wslot]])
                    for c2 in range(2):
                        fc = g * 2 + c2
                        for s in range(NS):
                            sl = slice(s * 512, (s + 1) * 512)
                            pg, pu = 1 + (fc * NS + s) % 2, 3 + (fc * NS + s) % 2
                            for (pp, gi) in ((pg, 0), (pu, 1)):
                                for kc in range(8):
                                    k.op("pe", lambda e, pp=pp, gi=gi, kc=kc, c2=c2, sl=sl, wslot=wslot: e.matmul(
                                        ps[pp][:, :], lhsT=wgu[wslot][:, gi, kc, c2 * 128:(c2 + 1) * 128],
                                        rhs=xn[:, kc, sl], start=(kc == 0), stop=(kc == 7)),
                                         reads=[r_wgu[wslot], r_xn[s]], writes=[r_ps[pp]])
                            si = (fc * NS + s) % 2
                            k.op("act", lambda e, si=si, pg=pg: e.activation(out=sg[si][:], in_=ps[pg][:, :], func=AF.Silu),
                                 reads=[r_ps[pg]], writes=[r_sg[si]])
                            k.op("dve", lambda e, si=si, pu=pu, fc=fc, sl=sl: e.tensor_tensor(
                                out=act[:, fc, sl], in0=sg[si][:], in1=ps[pu][:, :], op=ALU.mult),
                                 reads=[r_sg[si], r_ps[pu]], writes=[r_act[fc][s]])
                if t + 1 < ntiles:
                    if first:
                        load_tile(t + 1)
                    norm_tile(t + 1)
                for dc in range(8):
                    wslot = wq["d"] % 2
                    wq["d"] += 1
                    k.dma("sp", tl_wd[wslot], wd[wslot][:], wd_d[dc], reads=[self.cv_d[(l, f, dc)]], writes=[r_wd[wslot]])
                    for s in range(NS):
                        sl = slice(s * 512, (s + 1) * 512)
                        pd = 5 + (dc * NS + s) % 2
                        for fc in range(NFC):
                            k.op("pe", lambda e, pd=pd, fc=fc, sl=sl, wslot=wslot: e.matmul(
                                ps[pd][:, :], lhsT=wd[wslot][:, fc, :], rhs=act[:, fc, sl],
                                start=(fc == 0), stop=(fc == NFC - 1)),
                                 reads=[r_wd[wslot], r_act[fc][s]], writes=[r_ps[pd]])
                        k.op("dve", lambda e, pd=pd, dc=dc, sl=sl, b=b: e.scalar_tensor_tensor(
                            out=h_sb[b][:, dc, sl], in0=ps[pd][:, :], scalar=0.5, in1=h_sb[b][:, dc, sl],
                            op0=ALU.mult, op1=ALU.add),
                             reads=[r_ps[pd], r_h[b]], writes=[r_h[b]])
                if not last:
                    k.dma("sp", tl_st[b], self.hbuf[:, :, t * TF:(t + 1) * TF], h_sb[b][:],
                          reads=[r_h[b]], writes=[r_hb[t]])
                else:
                    fwi = self.normw[("final_norm", None)]
                    for s in range(NS):
                        sl = slice(s * 512, (s + 1) * 512)
                        ofm = h_sb[b][:, :, sl]
                        r_ofm = r_h[b]
                        self.rmsnorm(h_sb[b][:, :, sl], r_h[b], fwi, lambda c, ofm=ofm: ofm[:, c, :], r_ofm,
                                     sq, r_sq, ps[0], r_ps[0], rstd, r_rstd, 512)
                        for blk in range(4):
                            ob = blk % 2
                            t0 = t * TF + s * 512 + blk * 128
                            for half in range(2):
                                pi = 1 + half
                                for c4 in range(4):
                                    c = half * 4 + c4
                                    k.op("pe", lambda e, c=c, c4=c4, pi=pi, blk=blk: e.transpose(
                                        out=ps[pi][:, c4 * 128:(c4 + 1) * 128], in_=ofm[:, c, blk * 128:(blk + 1) * 128],
                                        identity=self.ident_f),
                                         reads=[r_ofm, self.r_const], writes=[r_ps[pi]])
                                if half == 0:
                                    k.op("act", lambda e, ob=ob, pi=pi: e.activation(
                                        out=otok[ob][:, 0:512], in_=ps[pi][:, :], func=AF.Copy),
                                         reads=[r_ps[pi]], writes=[r_otok[ob]])
                                else:
                                    k.op("dve", lambda e, ob=ob, pi=pi: e.tensor_copy(
                                        out=otok[ob][:, 512:1024], in_=ps[pi][:, :]),
                                         reads=[r_ps[pi]], writes=[r_otok[ob]])
                            k.dma("sp", tl_otok[ob], self.out[t0:t0 + 128, :], otok[ob][:],
                                  reads=[r_otok[ob]], writes=[self.r_out])
            k.barrier()

    def build(self):
        k = self.k
        self.declare()
        TFm = 1024 if self.L >= 1024 else self.L
        self.r_hbuf = [k.res(f"hb{t}") for t in range(self.L // TFm)]
        self.r_out = k.res("out")
        self.setup_consts()
        nl = self.n_layers
        self.convert_layer(0)
        for l in range(nl):
            self.ffn_phase(l, 1, first=(l == 0), last=False)
            if self.do_mixer:
                self.mixer_phase(l)
            if l + 1 < nl:
                self.convert_layer(l + 1)
            self.ffn_phase(l, 2, first=False, last=(l == nl - 1))
        k.barrier()
        self.es.close()
        return self.nc

    def mixer_phase(self, l):
        k, nc, L = self.k, self.nc, self.L
        TM = 2048
        assert L % TM == 0
        ntiles = L // TM
        with ExitStack() as es:
            M = type("M", (), {})()
            self.M = M
            M.l, M.TM, M.es = l, TM, es
            M.u = self.sb(es, "u", [128, 8, TM], BF16)
            M.r_u = [k.res(f"u{s}") for s in range(4)]
            M.y = self.sb(es, "y", [128, 8, TM], BF16)
            M.r_y = [k.res(f"y{c}") for c in range(8)]
            M.tl_hp = [k.dma_tl(f"mhp{l}{i}") for i in range(2)]
            M.tl_hst = [k.dma_tl(f"mhst{l}{i}") for i in range(2)]
            M.tl_hq = [k.dma_tl(f"mhq{l}{i}") for i in range(2)]
            M.wsl = [self.sb(es, f"wsl{i}", [128, 8, 256], BF16) for i in range(2)]
            M.r_wsl = [k.res(f"wsl{i}") for i in range(2)]
            M.tl_wsl = [k.dma_tl(f"mw{l}{i}") for i in range(2)]
            M.wcnt = 0
            M.wsm = [self.sb(es, f"wsm{i}", [128, 8, 128], BF16) for i in range(3)]
            M.r_wsm = [k.res(f"wsm{i}") for i in range(3)]
            M.tl_wsm = [k.dma_tl(f"mwsm{l}{i}") for i in range(3)]
            M.ps = [self.pst(es, f"mps{i}", [128, 512], F32) for i in range(8)]
            M.r_ps = [k.res(f"mps{i}", excl=True) for i in range(8)]
            M.kTp1 = self.sb(es, "kTp1", [128, 2, 128], BF16)
            M.kTp2 = self.sb(es, "kTp2", [128, 2, 512], BF16)
            M.kTp3 = self.sb(es, "kTp3", [128, 2, TM], BF16)
            M.V1p = self.sb(es, "V1p", [128, 2, 128], BF16)
            M.V2p = self.sb(es, "V2p", [128, 2, 4, 128], BF16)
            M.V3p = self.sb(es, "V3p", [128, 2, 16, 128], BF16)
            M.r_prev = [k.res(f"aprev{i}") for i in range(2)]
            M.S = self.sb(es, "ssdS", [128, 512], F32)
            M.S_bf = self.sb(es, "ssdSb", [128, 512], BF16)
            M.ctail = self.sb(es, "ctail", [128, 8, 3], F32)
            M.wdt = self.sb(es, "wdt", [128, 8, 8], BF16)
            M.A_b = self.sb(es, "A_b", [128, 8], F32)
            M.r_S, M.r_Sbf, M.r_ctail, M.r_ssdc = k.res("S"), k.res("Sbf"), k.res("ctail"), k.res("ssdc")
            M.tl_misc = k.dma_tl(f"mmisc{l}")
            po = l * PK_L
            M.po = po
            k.op("pool", lambda e: e.memset(M.S[:], 0.0), writes=[M.r_S])
            k.op("pool", lambda e: e.memset(M.S_bf[:], 0.0), writes=[M.r_Sbf])
            k.op("pool", lambda e: e.memset(M.ctail[:], 0.0), writes=[M.r_ctail])
            k.dma("sp", M.tl_misc, M.wdt[:], self.win[l][:, :, 2304:2312], reads=[self.mixcv_res[l]], writes=[M.r_ssdc])
            k.op("act", lambda e: e.activation(out=M.A_b[:], in_=self.pk[:, po + 40:po + 48], func=AF.Exp),
                 reads=[self.r_const], writes=[M.r_ssdc])
            k.op("dve", lambda e: e.tensor_scalar(out=M.A_b[:], in0=M.A_b[:], scalar1=-1.0, scalar2=None, op0=ALU.mult),
                 reads=[M.r_ssdc], writes=[M.r_ssdc])
            M.diagW = self.sb(es, "diagW", [128, 4, 8, 128], BF16)
            for j_ in range(4):
                for c_ in range(8):
                    k.op("dve", lambda e, j_=j_, c_=c_: e.tensor_scalar(
                        out=M.diagW[:, j_, c_, :], in0=self.ident_f, scalar1=self.pk[:, po + j_ * 8 + c_:po + j_ * 8 + c_ + 1],
                        scalar2=None, op0=ALU.mult), reads=[self.r_const], writes=[M.r_ssdc])
            M.diagD = self.sb(es, "diagD", [128, 4, 128], BF16)
            for hp_ in range(4):
                k.op("dve", lambda e, hp_=hp_: e.tensor_scalar(
                    out=M.diagD[:, hp_, :], in0=self.ident_f, scalar1=self.pk[:, po + 56 + hp_:po + 57 + hp_], scalar2=None,
                    op0=ALU.mult), reads=[self.r_const], writes=[M.r_ssdc])
            M.HS = self.sb(es, "hgS", [128, 2, 128], F32)
            M.HS_bf = self.sb(es, "hgSb", [128, 2, 128], BF16)
            M.lb = self.sb(es, "hglb", [128, 2], F32)
            M.r_HS, M.r_HSb, M.r_lb = k.res("HS"), k.res("HSb"), k.res("lb")
            k.op("pool", lambda e: e.memset(M.HS[:], 0.0), writes=[M.r_HS])
            k.op("pool", lambda e: e.memset(M.HS_bf[:], 0.0), writes=[M.r_HSb])
            if l == 0:
                k.op("pool", lambda e: e.memset(M.lb[:], 1e-20), writes=[M.r_lb])
            else:
                k.op("dve", lambda e: e.tensor_tensor(out=M.lb[:], in0=self.pk[:, po + 66:po + 68], in1=self.pk[:, po + 68:po + 70],
                                                      op=ALU.subtract), reads=[self.r_const], writes=[M.r_lb])
                k.op("act", lambda e: e.activation(out=M.lb[:], in_=M.lb[:], func=AF.Exp), reads=[M.r_lb], writes=[M.r_lb])
                k.op("dve", lambda e: e.tensor_scalar(out=M.lb[:], in0=M.lb[:], scalar1=1.0, scalar2=None, op0=ALU.add),
                     reads=[M.r_lb], writes=[M.r_lb])
                k.op("dve", lambda e: e.reciprocal(out=M.lb[:], in_=M.lb[:]), reads=[M.r_lb], writes=[M.r_lb])
                k.op("dve", lambda e: e.tensor_scalar(out=M.lb[:], in0=M.lb[:], scalar1=1e-20, scalar2=1.0 - 1e-6,
                                                      op0=ALU.max, op1=ALU.min), reads=[M.r_lb], writes=[M.r_lb])
            nwi = self.normw[("mix_norm", l)]
            r_hb = self.r_hbuf
            import os
            DBG = int(os.environ.get("MIXDBG", "9"))
            def norm_gen(t, es2):
                hp = [self.sb(es2, f"hp{i}", [128, 8, 512], F32) for i in range(2)]
                r_hp = [k.res(f"hp{i}") for i in range(2)]
                sq = self.sb(es2, "msq", [128, 8, 512], BF16)
                r_sq = k.res("msq")
                rstd = self.sb(es2, "mrstd", [128, 512], F32)
                r_rstd = k.res("mrstd")

                def ld(s):
                    t0 = t * TM + s * 512
                    k.dma("sp", M.tl_hp[s % 2], hp[s % 2][:], self.hbuf[:, :, t0:t0 + 512],
                          reads=[r_hb[t0 // 1024]], writes=[r_hp[s % 2]])
                ld(0)
                ld(1)
                for s in range(4):
                    sl = slice(s * 512, (s + 1) * 512)
                    self.rmsnorm(hp[s % 2][:, :, :], r_hp[s % 2], nwi, lambda c, sl=sl: M.u[:, c, sl], M.r_u[s],
                                 sq, r_sq, M.ps[s % 2], M.r_ps[s % 2], rstd, r_rstd, 512)
                    if s + 2 < 4:
                        ld(s + 2)
                    yield

            def wout_gen(t, es2):
                hq = [self.sb(es2, f"hq{i}", [128, 8, 512], F32) for i in range(2)]
                r_hq = [k.res(f"hq{i}") for i in range(2)]

                def ld2(s):
                    t0 = t * TM + s * 512
                    k.dma("sp", M.tl_hq[s % 2], hq[s % 2][:], self.hbuf[:, :, t0:t0 + 512],
                          reads=[r_hb[t0 // 1024]], writes=[r_hq[s % 2]])
                ld2(0)
                ld2(1)
                for s in range(4):
                    t0 = t * TM + s * 512
                    sl = slice(s * 512, (s + 1) * 512)
                    hb, r_hbb = hq[s % 2], r_hq[s % 2]
                    for dc2 in range(4):
                        w, r_w = self.mix_wload(self.wout[l][:, :, dc2 * 256:(dc2 + 1) * 256], 256)
                        for d2 in range(2):
                            dc = dc2 * 2 + d2
                            pi = 4 + dc % 4
                            for kc in range(8):
                                k.op("pe", lambda e, pi=pi, kc=kc, d2=d2, sl=sl, w=w: e.matmul(
                                    M.ps[pi][:, :], lhsT=w[:, kc, d2 * 128:(d2 + 1) * 128], rhs=M.y[:, kc, sl],
                                    start=(kc == 0), stop=(kc == 7)),
                                     reads=[r_w, M.r_y[kc]], writes=[M.r_ps[pi]])
                            k.op("dve", lambda e, pi=pi, dc=dc, hb=hb: e.scalar_tensor_tensor(
                                out=hb[:, dc, :], in0=M.ps[pi][:, :], scalar=1.0, in1=hb[:, dc, :],
                                op0=ALU.mult, op1=ALU.add),
                                 reads=[M.r_ps[pi], r_hbb], writes=[r_hbb])
                    k.dma("sp", M.tl_hst[s % 2], self.hbuf[:, :, t0:t0 + 512], hb[:], reads=[r_hbb], writes=[r_hb[t0 // 1024]])
                    if s + 2 < 4:
                        ld2(s + 2)
                    yield

            with ExitStack() as es2:
                for _ in norm_gen(0, es2):
                    pass
                k.barrier()
            for t in range(ntiles):
                if "attn" in self.mix_parts:
                    self.attention(t)
                else:
                    k.op("dve", lambda e: e.memset(M.y[:, 0:2, :], 0.0), writes=M.r_y[0:2])
                if "ssd" in self.mix_parts:
                    self.ssd(t)
                else:
                    k.op("dve", lambda e: e.memset(M.y[:, 2:6, :], 0.0), writes=M.r_y[2:6])
                if "hgrn" in self.mix_parts:
                    self.hgrn(t)
                else:
                    k.op("dve", lambda e: e.memset(M.y[:, 6:8, :], 0.0), writes=M.r_y[6:8])
                with ExitStack() as es2:
                    gw = wout_gen(t, es2)
                    gn = norm_gen(t + 1, es2) if t + 1 < ntiles else None
                    for s in range(4):
                        if gn is not None:
                            next(gn, None)
                        next(gw, None)
                    for _ in gw:
                        pass
                    if gn is not None:
                        for _ in gn:
                            pass
                    k.barrier()
            k.barrier()

    def mix_wload_n(self, src_ap, ncols, i):
        k, M = self.k, self.M
        k.dma("sp", M.tl_wsm[i], M.wsm[i][:, :, 0:ncols], src_ap, reads=[self.mixcv_res[M.l]], writes=[M.r_wsm[i]])
        return M.wsm[i], M.r_wsm[i]

    def mix_wload(self, src_ap, ncols):
        k, M = self.k, self.M
        i = M.wcnt % 2
        M.wcnt += 1
        k.dma("sp", M.tl_wsl[i], M.wsl[i][:, :, 0:ncols], src_ap, reads=[self.mixcv_res[M.l]], writes=[M.r_wsl[i]])
        return M.wsl[i], M.r_wsl[i]

    def ssd(self, t):
        k, M, l = self.k, self.M, self.M.l
        TM, po = M.TM, M.po
        ps, r_ps = M.ps, M.r_ps
        pk = self.pk
        with ExitStack() as es:
            D2 = lambda n, shp, dt: [self.sb(es, f"{n}{i}", shp, dt) for i in range(2)]
            zs = D2("zs", [128, 4, 512], BF16)
            cin = self.sb(es, "cin", [128, 8, 515], BF16)
            xbcs = D2("xbcs", [128, 8, 512], BF16)
            dtr = self.sb(es, "dtr", [128, 4, 8], F32)
            dts = D2("dts", [128, 4, 8], F32)
            a_sb = D2("a_sb", [128, 4, 8], F32)
            TriH2 = D2("TriH", [128, 8, 128], BF16)
            TriL2 = D2("TriL", [128, 8, 128], BF16)
            a_hi = D2("a_hi", [128, 4, 8], BF16)
            a_lo = D2("a_lo", [128, 4, 8], BF16)
            acs = self.sb(es, "acs", [128, 8], F32)
            ndec = self.sb(es, "ndec", [128, 8], F32)
            nacs = self.sb(es, "nacs", [128, 8], F32)
            dec_s = self.sb(es, "dec_s", [128, 8], F32)
            cdec = D2("cdec", [128, 8], F32)
            seg = self.sb(es, "seg", [128, 8, 128], F32)
            LT = self.sb(es, "LT", [128, 8, 128], BF16)
            Eacs = self.sb(es, "Eacs", [128, 8, 128], F32)
            CsT = D2("CsT", [128, 8, 128], BF16)
            CBm = self.sb(es, "CBm", [128, 2, 128], BF16)
            MT = D2("MT", [128, 8, 128], BF16)
            xB = D2("xB", [128, 768], BF16)
            xdt = D2("xdt", [128, 8, 64], BF16)
            xdec = D2("xdec", [128, 8, 64], BF16)
            ysb1 = self.sb(es, "ysb", [128, 4, 512], F32)
            ysb = [ysb1, ysb1]
            tmpS = self.sb(es, "tmpS", [128, 512], F32)
            ysq = self.sb(es, "ysq", [128, 4, 512], BF16)[:, :, :]
            rs = seg[:, :, :].rearrange("p (g a) t -> p g (a t)", g=2)
            R = {}
            for n in ("cin", "dtr", "acs", "nacs", "ndec", "dec_s", "seg", "LT", "Eacs", "CBm", "tmpS"):
                R[n] = k.res(n)
            R["ysq"], R["rs"] = k.res("ysq"), R["seg"]
            for n in ("TriH", "TriL", "zs", "xbcs", "dts", "a_sb", "a_hi", "a_lo", "cdec", "CsT", "MT", "xB", "xdt", "xdec"):
                R[n] = [k.res(n + "0"), k.res(n + "1")]
            r_ysb1 = k.res("ysb")
            R["ysb"] = [r_ysb1, r_ysb1]
            k.op("pool", lambda e: e.tensor_copy(out=cin[:, :, 0:3], in_=M.ctail[:, :, :]), reads=[M.r_ctail], writes=[R["cin"]])

            def prologue(s):
                sb_ = s % 2
                sl = slice(s * 512, (s + 1) * 512)
                for c2 in range(6):
                    w, r_w = self.mix_wload(self.win[l][:, :, 768 + c2 * 256:768 + (c2 + 1) * 256], 256)
                    for d2 in range(2):
                        c = c2 * 2 + d2
                        pi = 2 + c % 2
                        if c > 0:
                            yield
                        for kc in range(8):
                            k.op("pe", lambda e, pi=pi, kc=kc, d2=d2, w=w: e.matmul(
                                ps[pi][:, :], lhsT=w[:, kc, d2 * 128:(d2 + 1) * 128], rhs=M.u[:, kc, sl],
                                start=(kc == 0), stop=(kc == 7)),
                                 reads=[r_w, M.r_u[s]], writes=[r_ps[pi]])
                        if c < 4:
                            k.op("act", lambda e, pi=pi, c=c: e.activation(out=zs[sb_][:, c, :], in_=ps[pi][:, :], func=AF.Silu),
                                 reads=[r_ps[pi]], writes=[R["zs"][sb_]])
                        else:
                            k.op("act", lambda e, pi=pi, c=c: e.activation(out=cin[:, c - 4, 3:515], in_=ps[pi][:, :], func=AF.Copy),
                                 reads=[r_ps[pi]], writes=[R["cin"]])
                for c in range(8):
                    yield
                    pi = 2 + c % 2
                    for j in range(4):
                        k.op("pe", lambda e, c=c, j=j, pi=pi: e.matmul(
                            ps[pi][:, :], lhsT=M.diagW[:, j, c, :], rhs=cin[:, c, j:j + 512], start=(j == 0), stop=(j == 3)),
                             reads=[R["cin"], M.r_ssdc], writes=[r_ps[pi]])
                    k.op("act", lambda e, c=c, pi=pi: e.activation(
                        out=xbcs[sb_][:, c, :], in_=ps[pi][:, :], func=AF.Silu, bias=pk[:, po + 32 + c:po + 33 + c], scale=1.0),
                         reads=[r_ps[pi], self.r_const], writes=[R["xbcs"][sb_]])
                k.op("pool", lambda e: e.tensor_copy(out=cin[:, :, 0:3], in_=cin[:, :, 512:515]),
                     reads=[R["cin"]], writes=[R["cin"]])
                yield
                for j in range(4):
                    tk = slice(s * 512 + j * 128, s * 512 + (j + 1) * 128)
                    for kc in range(8):
                        k.op("pe", lambda e, j=j, kc=kc, tk=tk: e.matmul(
                            ps[7][:, 400 + j * 8:400 + (j + 1) * 8], lhsT=M.u[:, kc, tk], rhs=M.wdt[:, kc, :],
                            start=(kc == 0), stop=(kc == 7)),
                             reads=[M.r_ssdc, M.r_u[s]], writes=[r_ps[7]])
                k.op("dve", lambda e: e.tensor_tensor(
                    out=dtr[:], in0=pk[:, None, po + 48:po + 56].broadcast_to([128, 4, 8]),
                    in1=ps[7][:, 400:432].rearrange("p (j e) -> p j e", e=8), op=ALU.add),
                     reads=[r_ps[7], self.r_const], writes=[R["dtr"]])
                k.op("act", lambda e: e.activation(out=dtr[:], in_=dtr[:], func=AF.Exp), reads=[R["dtr"]], writes=[R["dtr"]])
                k.op("act", lambda e: e.activation(out=dts[sb_][:], in_=dtr[:], func=AF.Ln, bias=1.0, scale=1.0),
                     reads=[R["dtr"]], writes=[R["dts"][sb_]])
                k.op("dve", lambda e: e.tensor_tensor(
                    out=a_sb[sb_][:], in0=dts[sb_][:], in1=M.A_b[:, None, :].broadcast_to([128, 4, 8]), op=ALU.mult),
                     reads=[R["dts"][sb_], M.r_ssdc], writes=[R["a_sb"][sb_]])
                k.op("dve", lambda e: e.tensor_copy(out=a_hi[sb_][:], in_=a_sb[sb_][:]),
                     reads=[R["a_sb"][sb_]], writes=[R["a_hi"][sb_]])
                k.op("dve", lambda e: e.tensor_tensor(out=a_lo[sb_][:], in0=a_sb[sb_][:], in1=a_hi[sb_][:], op=ALU.subtract),
                     reads=[R["a_sb"][sb_], R["a_hi"][sb_]], writes=[R["a_lo"][sb_]])

            def tri(g):
                s, j, b = g // 4, g % 4, g % 2
                sb_ = s % 2
                k.op("dve", lambda e: e.tensor_tensor(
                    out=TriH2[b][:], in0=self.tri_b[:, None, :].broadcast_to([128, 8, 128]),
                    in1=a_hi[sb_][:, j, :, None].broadcast_to([128, 8, 128]), op=ALU.mult),
                     reads=[R["a_hi"][sb_], self.r_const], writes=[R["TriH"][b]])
                k.op("pool", lambda e: e.tensor_tensor(
                    out=TriL2[b][:], in0=self.tri_b[:, None, :].broadcast_to([128, 8, 128]),
                    in1=a_lo[sb_][:, j, :, None].broadcast_to([128, 8, 128]), op=ALU.mult),
                     reads=[R["a_lo"][sb_], self.r_const], writes=[R["TriL"][b]])

            def front(g):
                s, j, b = g // 4, g % 4, g % 2
                sb_ = s % 2
                cs = slice(j * 128, (j + 1) * 128)
                xb = xbcs[sb_]
                r_xb = R["xbcs"][sb_]
                if g % 4 == 0:
                    tri(g)
                TriH, TriL = TriH2[b], TriL2[b]
                pT = ps[4][:, 0:384].bitcast(BF16)
                for c in range(6):
                    k.op("pe", lambda e, c=c: e.transpose(
                        out=pT[:, c * 128:(c + 1) * 128], in_=xb[:, c, cs], identity=self.ident_b[:]),
                         reads=[r_xb, self.r_const], writes=[r_ps[4]])
                for hf in range(2):
                    for (ti, TT, rn) in ((0, TriH, "TriH"), (1, TriL, "TriL")):
                        k.op("pe", lambda e, hf=hf, ti=ti, TT=TT: e.matmul(
                            ps[5 + hf][:, :], lhsT=self.ones_b[:, :],
                            rhs=TT[:, hf * 4:(hf + 1) * 4, :].rearrange("p e t -> p (e t)"), start=(ti == 0), stop=(ti == 1)),
                             reads=[R[rn][b], self.r_const], writes=[r_ps[5 + hf]])
                if (g + 1) % 4 != 0:
                    tri(g + 1)
                k.op("pe", lambda e: e.matmul(ps[7][:, 64:72], lhsT=self.tri_f, rhs=a_sb[sb_][:, j, :], start=True, stop=True),
                     reads=[R["a_sb"][sb_], self.r_const], writes=[r_ps[7]])
                for gq in range(2):
                    k.op("pe", lambda e, gq=gq: e.matmul(
                        ps[7][:, 128 + gq * 128:256 + gq * 128], lhsT=xb[:, 4 + gq, cs], rhs=xb[:, 6 + gq, cs],
                        start=True, stop=True), reads=[r_xb], writes=[r_ps[7]])
                k.op("act", lambda e: e.activation(out=xB[b][:], in_=pT, func=AF.Copy),
                     reads=[r_ps[4]], writes=[R["xB"][b]])
                k.op("dve", lambda e: e.tensor_copy(out=acs[:], in_=ps[7][:, 64:72]), reads=[r_ps[7]], writes=[R["acs"]])
                k.op("dve", lambda e: e.tensor_tensor(
                    out=CBm[:], in0=self.tri_b[:, None, :].broadcast_to([128, 2, 128]),
                    in1=ps[7][:, 128:384].rearrange("p (g t) -> p g t", g=2), op=ALU.mult),
                     reads=[r_ps[7], self.r_const], writes=[R["CBm"]])
                for hf in range(2):
                    al = ps[5 + hf][:, :].rearrange("p (e t) -> p e t", t=128)[:, :, 127]
                    k.op("dve", lambda e, hf=hf, al=al: e.tensor_tensor(
                        out=ndec[:, hf * 4:(hf + 1) * 4], in0=acs[:, hf * 4:(hf + 1) * 4], in1=al, op=ALU.subtract),
                         reads=[R["acs"], r_ps[5 + hf]], writes=[R["ndec"]])
                k.op("dve", lambda e: e.tensor_scalar(out=nacs[:], in0=acs[:], scalar1=-1.0, scalar2=None, op0=ALU.mult),
                     reads=[R["acs"]], writes=[R["nacs"]])
                for e_ in range(8):
                    hf = e_ // 4
                    k.op("act", lambda e, e_=e_, hf=hf: e.activation(
                        out=seg[:, e_, :], in_=ps[5 + hf][:, (e_ % 4) * 128:(e_ % 4 + 1) * 128],
                        func=AF.Identity, bias=nacs[:, e_:e_ + 1], scale=1.0),
                         reads=[r_ps[5 + hf], R["nacs"]], writes=[R["seg"]])
                k.op("dve", lambda e: e.tensor_scalar(out=seg[:], in0=seg[:], scalar1=0.0, scalar2=None, op0=ALU.min),
                     reads=[R["seg"]], writes=[R["seg"]])
                for hf in range(2):
                    al = ps[5 + hf][:, :].rearrange("p (e t) -> p e t", t=128)[:, :, 127]
                    k.op("act", lambda e, hf=hf, al=al: e.activation(out=cdec[b][:, hf * 4:(hf + 1) * 4], in_=al, func=AF.Exp),
                         reads=[r_ps[5 + hf]], writes=[R["cdec"][b]])
                    k.op("act", lambda e, hf=hf: e.activation(
                        out=Eacs[:, hf * 4:(hf + 1) * 4, :], in_=ps[5 + hf][:, :].rearrange("p (e t) -> p e t", t=128),
                        func=AF.Exp), reads=[r_ps[5 + hf]], writes=[R["Eacs"]])
                k.op("act", lambda e: e.activation(out=dec_s[:], in_=ndec[:], func=AF.Exp, scale=-1.0),
                     reads=[R["ndec"]], writes=[R["dec_s"]])
                k.op("act", lambda e: e.activation(out=LT[:], in_=seg[:], func=AF.Exp), reads=[R["seg"]], writes=[R["LT"]])

            def front_b(g):
                s, j, b = g // 4, g % 4, g % 2
                sb_ = s % 2
                cs = slice(j * 128, (j + 1) * 128)
                xb = xbcs[sb_]
                r_xb = R["xbcs"][sb_]
                for gq in range(2):
                    k.op("pool", lambda e, gq=gq: e.tensor_tensor(
                        out=CsT[b][:, gq * 4:(gq + 1) * 4, :], in0=Eacs[:, gq * 4:(gq + 1) * 4, :],
                        in1=xb[:, 6 + gq:7 + gq, cs].broadcast_to([128, 4, 128]), op=ALU.mult),
                         reads=[R["Eacs"], r_xb], writes=[R["CsT"][b]])
                k.op("pool", lambda e: e.tensor_tensor(
                    out=xdt[b][:], in0=xB[b][:, 0:512].rearrange("p (e q) -> p e q", q=64),
                    in1=dts[sb_][:, j, :, None].broadcast_to([128, 8, 64]), op=ALU.mult),
                     reads=[R["xB"][b], R["dts"][sb_]], writes=[R["xdt"][b]])
                k.op("pool", lambda e: e.tensor_tensor(
                    out=xdec[b][:], in0=xdt[b][:], in1=dec_s[:, :, None].broadcast_to([128, 8, 64]), op=ALU.mult),
                     reads=[R["xdt"][b], R["dec_s"]], writes=[R["xdec"][b]])
                for gq in range(2):
                    k.op("dve", lambda e, gq=gq: e.tensor_tensor(
                        out=MT[b][:, gq * 4:(gq + 1) * 4, :], in0=LT[:, gq * 4:(gq + 1) * 4, :],
                        in1=CBm[:, gq:gq + 1, :].broadcast_to([128, 4, 128]), op=ALU.mult),
                         reads=[R["LT"], R["CBm"]], writes=[R["MT"][b]])

            def back(g):
                s, j, b = g // 4, g % 4, g % 2
                sb_ = s % 2
                cs = slice(j * 128, (j + 1) * 128)
                xb = xbcs[sb_]
                r_xb = R["xbcs"][sb_]
                for e_ in range(8):
                    hp, pb = e_ // 2, e_ // 4
                    col = (e_ % 4) * 128
                    k.op("pe", lambda e, e_=e_, hp=hp, pb=pb, col=col: e.matmul(
                        ps[pb][:, col:col + 128], lhsT=xdt[b][:, 2 * hp:2 * hp + 2, :].rearrange("p e q -> p (e q)"),
                        rhs=MT[b][:, e_, :], start=True, stop=False),
                         reads=[R["xdt"][b], R["MT"][b]], writes=[r_ps[pb]])
                    k.op("pe", lambda e, e_=e_, hp=hp, pb=pb, col=col: e.matmul(
                        ps[pb][:, col:col + 128], lhsT=M.diagD[:, hp, :], rhs=xb[:, hp, cs], start=False, stop=False),
                         reads=[M.r_ssdc, r_xb], writes=[r_ps[pb]])
                    k.op("pe", lambda e, e_=e_, hp=hp, pb=pb, col=col: e.matmul(
                        ps[pb][:, col:col + 128], lhsT=M.S_bf[:, hp * 128:(hp + 1) * 128],
                        rhs=CsT[b][:, e_, :], start=False, stop=True),
                         reads=[M.r_Sbf, R["CsT"][b]], writes=[r_ps[pb]])
                for gq in range(2):
                    k.op("pe", lambda e, gq=gq: e.matmul(
                        ps[4][:, gq * 256:(gq + 1) * 256], lhsT=xB[b][:, 512 + gq * 128:512 + (gq + 1) * 128],
                        rhs=xdec[b][:, gq * 4:(gq + 1) * 4, :].rearrange("p e q -> p (e q)"), start=True, stop=True),
                         reads=[R["xB"][b], R["xdec"][b]], writes=[r_ps[4]])
                k.op("pool", lambda e: e.tensor_tensor(
                    out=tmpS[:, :].rearrange("p (e q) -> p e q", q=64), in0=M.S[:, :].rearrange("p (e q) -> p e q", q=64),
                    in1=cdec[b][:, :, None].broadcast_to([128, 8, 64]), op=ALU.mult),
                     reads=[R["cdec"][b], M.r_S], writes=[R["tmpS"]])
                k.op("dve", lambda e: e.tensor_tensor(out=M.S_bf[:], in0=tmpS[:], in1=ps[4][:, :], op=ALU.add),
                     reads=[R["tmpS"], r_ps[4]], writes=[M.r_Sbf])
                k.op("dve", lambda e: e.tensor_tensor(out=M.S[:], in0=tmpS[:], in1=ps[4][:, :], op=ALU.add),
                     reads=[R["tmpS"], r_ps[4]], writes=[M.r_S])
                for pb in range(2):
                    for par in range(2):
                        rows = slice(64 * par, 64 * par + 64)
                        src = ps[pb][rows, :].rearrange("p (h q t) -> p h q t", q=2, t=128)[:, :, par, :]
                        dst = ysb[sb_][rows, 2 * pb:2 * pb + 2, cs]
                        if par == 0:
                            k.op("act", lambda e, src=src, dst=dst: e.activation(out=dst, in_=src, func=AF.Copy),
                                 reads=[r_ps[pb]], writes=[R["ysb"][sb_]])
                        else:
                            k.op("dve", lambda e, src=src, dst=dst: e.tensor_copy(out=dst, in_=src),
                                 reads=[r_ps[pb]], writes=[R["ysb"][sb_]])

            def epilogue_a(s):
                sb_ = s % 2
                yb, r_yb = ysb[sb_], R["ysb"][sb_]
                k.op("dve", lambda e: e.tensor_tensor(out=yb[:], in0=yb[:], in1=zs[sb_][:], op=ALU.mult),
                     reads=[r_yb, R["zs"][sb_]], writes=[r_yb])
                k.op("act", lambda e: e.activation(out=ysq, in_=yb[:], func=AF.Square), reads=[r_yb], writes=[R["ysq"]])

            def epilogue_b(s):
                sb_ = s % 2
                sl = slice(s * 512, (s + 1) * 512)
                yb, r_yb = ysb[sb_], R["ysb"][sb_]
                for gq in range(2):
                    for c2 in range(2):
                        k.op("pe", lambda e, gq=gq, c2=c2: e.matmul(
                            ps[2 + gq][:, :], lhsT=self.ones_b[:, :], rhs=ysq[:, 2 * gq + c2, :], start=(c2 == 0), stop=(c2 == 1)),
                             reads=[R["ysq"], self.r_const], writes=[r_ps[2 + gq]])
                    k.op("act", lambda e, gq=gq: e.activation(out=rs[:, gq, :], in_=ps[2 + gq][:, :], func=AF.Sqrt,
                                                          bias=float(EPS), scale=1.0 / 256.0),
                         reads=[r_ps[2 + gq]], writes=[R["rs"]])
                k.op("dve", lambda e: e.reciprocal(out=rs, in_=rs), reads=[R["rs"]], writes=[R["rs"]])
                for c in range(4):
                    k.op("dve", lambda e, c=c: e.scalar_tensor_tensor(
                        out=M.y[:, 2 + c, sl], in0=yb[:, c, :], scalar=pk[:, po + 60 + c:po + 61 + c], in1=rs[:, c // 2, :],
                        op0=ALU.mult, op1=ALU.mult),
                         reads=[r_yb, R["rs"], self.r_const], writes=[M.r_y[2 + c]])

            for _ in prologue(0):
                pass
            pro = [None]
            pend = [None]

            def pro_step(n):
                for _ in range(n):
                    if pro[0] is None:
                        return
                    try:
                        next(pro[0])
                    except StopIteration:
                        pro[0] = None

            for g in range(16):
                s_ = g // 4
                if g % 4 == 0:
                    pro_step(1000)
                front(g)
                if pend[0] is not None:
                    epilogue_b(pend[0])
                    pend[0] = None
                    if g % 4 == 1 and s_ < 3:
                        pro[0] = prologue(s_ + 1)
                pro_step(2)
                if g > 0:
                    back(g - 1)
                    pro_step(2)
                front_b(g)
                if g > 0 and (g - 1) % 4 == 3:
                    epilogue_a((g - 1) // 4)
                    pend[0] = (g - 1) // 4
                if g == 0:
                    pro[0] = prologue(1)
                pro_step(3)
            back(15)
            epilogue_a(3)
            epilogue_b(3)
            k.op("pool", lambda e: e.tensor_copy(out=M.ctail[:, :, :], in_=cin[:, :, 0:3]), reads=[R["cin"]], writes=[M.r_ctail])
            k.barrier()

    def hgrn(self, t):
        k, M, l = self.k, self.M, self.M.l
        TM, po = M.TM, M.po
        ps, r_ps = M.ps, M.r_ps
        pk = self.pk
        with ExitStack() as es:
            f4 = lambda n: self.sb(es, n, [128, 2, 512], F32)
            b4 = lambda n: self.sb(es, n, [128, 2, 512], BF16)
            E, L1, LF, KK, Q, BC, TMP, EX = f4("hE"), f4("hL1"), f4("hLF"), f4("hKK"), f4("hQ"), f4("hBC"), f4("hTMP"), f4("hEX")
            G = [b4("hG0"), b4("hG1")]
            Qt = [b4("hQt0"), b4("hQt1")]
            Kt = [b4("hKt0"), b4("hKt1")]
            Qh = [b4("hQh0"), b4("hQh1")]
            Kh = [b4("hKh0"), b4("hKh1")]
            edec = [self.sb(es, f"hedec{i}", [128, 2, 8], F32) for i in range(2)]
            osb = f4("hosb")
            osq = b4("hosq")
            rs = f4("hrs")
            Vz = [self.sb(es, f"hVz{i}", [64, 2, 2, 128], BF16) for i in range(2)]
            AT = [self.sb(es, f"hAT{i}", [64, 2, 2, 64], BF16) for i in range(2)]
            KhT = [self.sb(es, f"hKhT{i}", [64, 2, 128], BF16) for i in range(2)]
            R = {n: k.res(n) for n in ("E", "L1", "LF", "KK", "Q", "BC", "TMP", "EX", "osb", "osq", "rs")}
            for n in ("G", "Qt", "Kt", "Qh", "Kh", "edec", "Vz", "AT", "KhT"):
                R[n] = [k.res(n + "0"), k.res(n + "1")]
            for i in range(2):
                k.op("pool", lambda e, i=i: e.memset(Vz[i][:], 0.0), writes=[R["Vz"][i]])
            whi, r_whi = self.mix_wload_n(self.win[l][:, :, 2824:2952], 128, 0)
            whi2, r_whi2 = self.mix_wload_n(self.win[l][:, :, 2952:3080], 128, 1)

            def prologue(s):
                sb_ = s % 2
                sl = slice(s * 512, (s + 1) * 512)
                first = True
                for (c0, kind) in ((2568, "f"), (2312, "q"), (3080, "g")):
                    w, r_w = self.mix_wload(self.win[l][:, :, c0:c0 + 256], 256)
                    for c in range(2):
                        if not first:
                            yield
                        first = False
                        pi = 2 + c
                        for kc in range(8):
                            k.op("pe", lambda e, pi=pi, kc=kc, c=c, w=w: e.matmul(
                                ps[pi][:, :], lhsT=w[:, kc, c * 128:(c + 1) * 128], rhs=M.u[:, kc, sl],
                                start=(kc == 0), stop=(kc == 7)),
                                 reads=[r_w, M.r_u[s]], writes=[r_ps[pi]])
                        if kind == "q":
                            k.op("act", lambda e, pi=pi, c=c: e.activation(out=Q[:, c, :], in_=ps[pi][:, :], func=AF.Silu),
                                 reads=[r_ps[pi]], writes=[R["Q"]])
                        elif kind == "g":
                            k.op("act", lambda e, pi=pi, c=c: e.activation(out=G[sb_][:, c, :], in_=ps[pi][:, :], func=AF.Silu),
                                 reads=[r_ps[pi]], writes=[R["G"][sb_]])
                        else:
                            k.op("act", lambda e, pi=pi, c=c: e.activation(out=E[:, c, :], in_=ps[pi][:, :], func=AF.Exp, scale=-1.0),
                                 reads=[r_ps[pi]], writes=[R["E"]])
                yield
                for c in range(2):
                    k.op("act", lambda e, c=c: e.activation(out=L1[:, c, :], in_=E[:, c, :], func=AF.Ln, bias=1.0,
                                                          scale=M.lb[:, c:c + 1]),
                         reads=[R["E"], M.r_lb], writes=[R["L1"]])
                k.op("act", lambda e: e.activation(out=LF[:], in_=E[:], func=AF.Ln, bias=1.0, scale=1.0),
                     reads=[R["E"]], writes=[R["LF"]])
                k.op("pool", lambda e: e.tensor_tensor(out=LF[:], in0=L1[:], in1=LF[:], op=ALU.subtract),
                     reads=[R["L1"], R["LF"]], writes=[R["LF"]])
                yield
                k.op("act", lambda e: e.activation(out=KK[:], in_=LF[:], func=AF.Exp), reads=[R["LF"]], writes=[R["KK"]])
                k.op("pool", lambda e: e.tensor_scalar(out=KK[:], in0=KK[:], scalar1=-1.0, scalar2=1.0, op0=ALU.mult, op1=ALU.add),
                     reads=[R["KK"]], writes=[R["KK"]])
                for c in range(2):
                    k.op("dve", lambda e, c=c: e.tensor_tensor_scan(
                        out=BC[:, c, :], data0=self.rst_f, data1=LF[:, c, :], initial=0.0, op0=ALU.mult, op1=ALU.add),
                         reads=[R["LF"], self.r_const], writes=[R["BC"]])
                yield
                BC4 = BC[:, :, :].rearrange("p c (j t) -> p c j t", t=64)
                TMP4 = TMP[:, :, :].rearrange("p c (j t) -> p c j t", t=64)
                for c in range(2):
                    k.op("pool", lambda e, c=c: e.tensor_tensor(
                        out=TMP4[:, c], in0=BC4[:, c], in1=BC4[:, c, :, 31:32].broadcast_to([128, 8, 64]), op=ALU.subtract),
                         reads=[R["BC"]], writes=[R["TMP"]])
                k.op("act", lambda e: e.activation(out=EX[:], in_=TMP[:], func=AF.Exp), reads=[R["TMP"]], writes=[R["EX"]])
                k.op("dve", lambda e: e.tensor_tensor(out=Qt[sb_][:], in0=Q[:], in1=EX[:], op=ALU.mult),
                     reads=[R["Q"], R["EX"]], writes=[R["Qt"][sb_]])
                yield
                k.op("act", lambda e: e.activation(out=EX[:], in_=TMP[:], func=AF.Exp, scale=-1.0), reads=[R["TMP"]], writes=[R["EX"]])
                k.op("dve", lambda e: e.tensor_tensor(out=Kt[sb_][:], in0=KK[:], in1=EX[:], op=ALU.mult),
                     reads=[R["KK"], R["EX"]], writes=[R["Kt"][sb_]])
                yield
                k.op("act", lambda e: e.activation(out=EX[:], in_=BC[:], func=AF.Exp), reads=[R["BC"]], writes=[R["EX"]])
                k.op("dve", lambda e: e.tensor_tensor(out=Qh[sb_][:], in0=Q[:], in1=EX[:], op=ALU.mult),
                     reads=[R["Q"], R["EX"]], writes=[R["Qh"][sb_]])
                yield
                for c in range(2):
                    k.op("pool", lambda e, c=c: e.tensor_tensor(
                        out=TMP4[:, c], in0=BC4[:, c, :, 63:64].broadcast_to([128, 8, 64]), in1=BC4[:, c], op=ALU.subtract),
                         reads=[R["BC"]], writes=[R["TMP"]])
                    k.op("act", lambda e, c=c: e.activation(out=edec[sb_][:, c, :], in_=BC4[:, c, :, 63], func=AF.Exp),
                         reads=[R["BC"]], writes=[R["edec"][sb_]])
                k.op("act", lambda e: e.activation(out=EX[:], in_=TMP[:], func=AF.Exp), reads=[R["TMP"]], writes=[R["EX"]])
                k.op("dve", lambda e: e.tensor_tensor(out=Kh[sb_][:], in0=KK[:], in1=EX[:], op=ALU.mult),
                     reads=[R["KK"], R["EX"]], writes=[R["Kh"][sb_]])

            def stageA(g):
                s, j, b = g // 8, g % 8, g % 2
                sb_ = s % 2
                tk = slice(s * 512 + j * 64, s * 512 + (j + 1) * 64)
                cj = slice(j * 64, (j + 1) * 64)
                for (wv, r_wv, half) in ((whi, r_whi, 0), (whi2, r_whi2, 1)):
                    for kc in range(8):
                        k.op("pe", lambda e, kc=kc, wv=wv, half=half: e.matmul(
                            ps[4][0:64, half * 128:(half + 1) * 128], lhsT=M.u[:, kc, tk], rhs=wv[:, kc, 0:128],
                            start=(kc == 0), stop=(kc == 7)),
                             reads=[r_wv, M.r_u[s]], writes=[r_ps[4]])
                pK = ps[7][:, 0:128].bitcast(BF16)
                for c in range(2):
                    k.op("pe", lambda e, c=c: e.transpose(
                        out=pK[0:64, c * 128:(c + 1) * 128], in_=Kh[sb_][:, c, cj], identity=self.ident_b[:]),
                         reads=[R["Kh"][sb_], self.r_const], writes=[r_ps[7]])
                for c in range(2):
                    for par in range(2):
                        rows = slice(64 * par, 64 * par + 64)
                        k.op("pe", lambda e, c=c, par=par, rows=rows: e.matmul(
                            ps[5 + par][0:64, c * 64:(c + 1) * 64], lhsT=Kt[sb_][rows, c, cj], rhs=Qt[sb_][rows, c, cj],
                            start=True, stop=True),
                             reads=[R["Kt"][sb_], R["Qt"][sb_]], writes=[r_ps[5 + par]])
                for par in range(2):
                    k.op("dve", lambda e, par=par: e.tensor_copy(
                        out=Vz[b][:, :, par, par * 64:(par + 1) * 64],
                        in_=ps[4][0:64, 0:256].rearrange("s (c par v) -> s c par v", par=2, v=64)[:, :, par, :]),
                         reads=[r_ps[4]], writes=[R["Vz"][b]])
                k.op("act", lambda e: e.activation(
                    out=KhT[b][:, :, :], in_=pK[0:64, :].rearrange("s (c k) -> s c k", c=2), func=AF.Copy),
                     reads=[r_ps[7]], writes=[R["KhT"][b]])
                for par in range(2):
                    k.op("dve", lambda e, par=par: e.tensor_tensor(
                        out=AT[b][:, :, par, :], in0=self.tri_b[0:64, None, 0:64].broadcast_to([64, 2, 64]),
                        in1=ps[5 + par][0:64, 0:128].rearrange("s (c t) -> s c t", c=2), op=ALU.mult),
                         reads=[r_ps[5 + par], self.r_const], writes=[R["AT"][b]])

            def stageB(g):
                s, j, b = g // 8, g % 8, g % 2
                sb_ = s % 2
                cj = slice(j * 64, (j + 1) * 64)
                for c in range(2):
                    for par in range(2):
                        k.op("pe", lambda e, c=c, par=par: e.matmul(
                            ps[c][:, cj], lhsT=Vz[b][:, c, par, :], rhs=AT[b][:, c, par, :], start=(par == 0), stop=False),
                             reads=[R["Vz"][b], R["AT"][b]], writes=[r_ps[c]])
                    k.op("pe", lambda e, c=c: e.matmul(
                        ps[c][:, cj], lhsT=M.HS_bf[:, c, :], rhs=Qh[sb_][:, c, cj], start=False, stop=True),
                         reads=[M.r_HSb, R["Qh"][sb_]], writes=[r_ps[c]])
                for c in range(2):
                    for par in range(2):
                        k.op("pe", lambda e, c=c, par=par: e.matmul(
                            ps[4][:, 256 + c * 128:256 + (c + 1) * 128], lhsT=KhT[b][:, c, :], rhs=Vz[b][:, c, par, :],
                            start=(par == 0), stop=(par == 1)),
                             reads=[R["KhT"][b], R["Vz"][b]], writes=[r_ps[4]])
                for c in range(2):
                    for par in range(2):
                        rows = slice(64 * par, 64 * par + 64)
                        for (dst, r_dst) in ((M.HS_bf, M.r_HSb), (M.HS, M.r_HS)):
                            k.op("dve", lambda e, c=c, par=par, rows=rows, dst=dst: e.scalar_tensor_tensor(
                                out=dst[rows, c, par * 64:(par + 1) * 64], in0=M.HS[rows, c, par * 64:(par + 1) * 64],
                                scalar=edec[sb_][rows, c, j:j + 1],
                                in1=ps[4][rows, 256 + c * 128 + par * 64:256 + c * 128 + (par + 1) * 64],
                                op0=ALU.mult, op1=ALU.add),
                                 reads=[M.r_HS, R["edec"][sb_], r_ps[4]], writes=[r_dst])

            def epilogue(s):
                sb_ = s % 2
                sl = slice(s * 512, (s + 1) * 512)
                for c in range(2):
                    k.op("act", lambda e, c=c: e.activation(out=osb[:, c, :], in_=ps[c][:, :], func=AF.Copy),
                         reads=[r_ps[c]], writes=[R["osb"]])
                k.op("act", lambda e: e.activation(out=osq[:], in_=osb[:], func=AF.Square), reads=[R["osb"]], writes=[R["osq"]])
                for c in range(2):
                    k.op("pe", lambda e, c=c: e.matmul(ps[2 + c][:, :], lhsT=self.blk_b[:, :], rhs=osq[:, c, :], start=True, stop=True),
                         reads=[R["osq"], self.r_const], writes=[r_ps[2 + c]])
                    k.op("act", lambda e, c=c: e.activation(out=rs[:, c, :], in_=ps[2 + c][:, :], func=AF.Ln,
                                                          bias=float(EPS), scale=1.0 / 64.0),
                         reads=[r_ps[2 + c]], writes=[R["rs"]])
                k.op("act", lambda e: e.activation(out=rs[:], in_=rs[:], func=AF.Exp, scale=-0.5), reads=[R["rs"]], writes=[R["rs"]])
                k.op("pool", lambda e: e.tensor_tensor(out=rs[:], in0=rs[:], in1=G[sb_][:], op=ALU.mult),
                     reads=[R["rs"], R["G"][sb_]], writes=[R["rs"]])
                for c in range(2):
                    k.op("dve", lambda e, c=c: e.scalar_tensor_tensor(
                        out=M.y[:, 6 + c, sl], in0=osb[:, c, :], scalar=pk[:, po + 64 + c:po + 65 + c], in1=rs[:, c, :],
                        op0=ALU.mult, op1=ALU.mult),
                         reads=[R["osb"], R["rs"], self.r_const], writes=[M.r_y[6 + c]])

            for _ in prologue(0):
                pass
            pro = [None]

            def pro_step(n):
                for _ in range(n):
                    if pro[0] is None:
                        return
                    try:
                        next(pro[0])
                    except StopIteration:
                        pro[0] = None

            NG = 32
            for g in range(NG):
                s_ = g // 8
                if g % 8 == 0:
                    pro_step(1000)
                stageA(g)
                if g > 0:
                    stageB(g - 1)
                if g > 0 and (g - 1) % 8 == 7:
                    epilogue((g - 1) // 8)
                if g % 8 == 0 and s_ < 3:
                    pro[0] = prologue(s_ + 1)
                pro_step(2)
            stageB(NG - 1)
            epilogue(3)
            k.barrier()

    def attention(self, t):
        k, M, l = self.k, self.M, self.M.l
        TM = M.TM
        ps, r_ps = M.ps, M.r_ps
        with ExitStack() as es:
            X = [[self.sb(es, f"X{x}{p}", [128, TM], BF16) for p in range(3)] for x in range(3)]
            r_X = [[k.res(f"X{x}{p}") for p in range(3)] for x in range(3)]
            Vc = [self.sb(es, f"Vc{p}", [128, 16, 128], BF16) for p in range(3)]
            r_V = [k.res() for _ in range(3)]
            OD = self.sb(es, "OD", [128, 2, TM], F32)
            Oacc = OD[:, 0, :]
            den = OD[:, 1, :]
            r_acc = [[k.res("Oacc0"), k.res("Oacc1")], [k.res("den0"), k.res("den1")]]
            PT = [self.sb(es, f"PT{i}", [128, 512], BF16) for i in range(3)]
            r_PT = [k.res() for _ in range(3)]
            blk_tok = [
                lambda b: slice(128 * b, 128 * b + 128),
                lambda b: slice(512 * (b // 4) + (b % 4), 512 * (b // 4) + 512, 4),
                lambda b: slice(b, TM, 16),
            ]

            def dst_ap(buf, p, s):
                if p == 0:
                    return buf[:, 512 * s:512 * s + 512]
                if p == 1:
                    return buf[:, 512 * s:512 * s + 512].rearrange("p (r i) -> p i r", r=4)
                return buf[:, :].rearrange("p (r i) -> p i r", r=16)[:, 32 * s:32 * s + 32, :]

            def src_ap(psb, p):
                if p == 0:
                    return psb[:, :]
                return psb[:, :].rearrange("p (i r) -> p i r", r=(4 if p == 1 else 16))

            cnt = 0
            ev = 0
            for hp in range(2):
                ws = []
                for x in range(3):
                    ws.append(self.mix_wload_n(self.win[l][:, :, 256 * x + hp * 128:256 * x + hp * 128 + 128], 128, x))
                for s in range(4):
                    sl = slice(s * 512, (s + 1) * 512)
                    for x in range(3):
                        ww, r_ww = ws[x]
                        pi = 1 + (s * 3 + x) % 3
                        for kc in range(8):
                            k.op("pe", lambda e, pi=pi, kc=kc, ww=ww, sl=sl: e.matmul(
                                ps[pi][:, :], lhsT=ww[:, kc, 0:128], rhs=M.u[:, kc, sl], start=(kc == 0), stop=(kc == 7)),
                                 reads=[r_ww, M.r_u[s]], writes=[r_ps[pi]])
                        for p in range(3):
                            ev += 1
                            if ev % 2 == 0:
                                k.op("act", lambda e, pi=pi, x=x, p=p, s=s: e.activation(
                                    out=dst_ap(X[x][p], p, s), in_=src_ap(ps[pi], p), func=AF.Copy),
                                     reads=[r_ps[pi]], writes=[r_X[x][p]])
                            else:
                                k.op("dve", lambda e, pi=pi, x=x, p=p, s=s: e.tensor_copy(
                                    out=dst_ap(X[x][p], p, s), in_=src_ap(ps[pi], p)),
                                     reads=[r_ps[pi]], writes=[r_X[x][p]])
                import os
                ADBG = int(os.environ.get("ADBG", "9"))
                for p in range(3):
                    if ADBG < 1:
                        break
                    for b4 in range(4):
                        pi = 4 + (p * 4 + b4) % 2
                        pv = ps[pi][:, 0:256].bitcast(BF16)
                        for bb in range(4):
                            b = b4 * 4 + bb
                            k.op("pe", lambda e, pv=pv, bb=bb, b=b, p=p: e.transpose(
                                out=pv[:, bb * 128:(bb + 1) * 128], in_=X[2][p][:, 128 * b:128 * b + 128],
                                identity=self.ident_b[:]),
                                 reads=[r_X[2][p], self.r_const], writes=[r_ps[pi]])
                        k.op("dve", lambda e, pv=pv, p=p, b4=b4: e.tensor_copy(
                            out=Vc[p][:, b4 * 4:(b4 + 1) * 4, :], in_=pv.rearrange("p (b c) -> p b c", b=4)),
                             reads=[r_ps[pi]], writes=[r_V[p]])
                blocks = [(p, b) for p in range(3) for b in range(16)]
                info = {}

                def stageA(i):
                    p, b = blocks[i]
                    tk = blk_tok[p](b)
                    bs = slice(128 * b, 128 * b + 128)
                    pb = b - (1, 4, 16)[p]
                    if pb >= 0:
                        prev = (X[1][p], slice(128 * pb, 128 * pb + 128), Vc[p][:, pb, :], [r_X[1][p], r_V[p]])
                    elif t > 0:
                        kp = (M.kTp1, M.kTp2, M.kTp3)[p]
                        vp = (M.V1p[:, hp, :], M.V2p[:, hp, b % 4, :], M.V3p[:, hp, b, :])[p]
                        off = (0, 128 * (b % 4), 128 * b)[p]
                        prev = (kp[:, hp, :], slice(off, off + 128), vp, [M.r_prev[hp]])
                    else:
                        prev = None
                    cur = (X[1][p], bs, Vc[p][:, b, :], [r_X[1][p], r_V[p]])
                    sp_ = prev if prev is not None else cur
                    pSs = (2 + i % 3, 5 + i % 3)
                    pt, r_pt = PT[i % 3], r_PT[i % 3]
                    info[i] = (p, tk, prev, cur, pt, r_pt)
                    mo = 0 if prev is not None else 512
                    for par in range(2):
                        pS = pSs[par]
                        k.op("pe", lambda e, pS=pS, mo=mo: e.matmul(
                            ps[pS][:, 0:256], lhsT=self.ident_b[:, :], rhs=self.mbias_b[:, mo:mo + 256], start=True, stop=False),
                             reads=[self.r_const], writes=[r_ps[pS]])
                    for par in range(2):
                        rows = slice(64 * par, 64 * par + 64)
                        pS = pSs[par]
                        for j, kb in enumerate((sp_, cur)):
                            k.op("pe", lambda e, pS=pS, j=j, kb=kb, rows=rows, bs=bs, p=p: e.matmul(
                                ps[pS][:, j * 128:(j + 1) * 128],
                                lhsT=kb[0][rows, kb[1]], rhs=X[0][p][rows, bs], start=False, stop=(j == 1)),
                                 reads=kb[3] + [r_X[0][p]], writes=[r_ps[pS]])
                    for par in range(2):
                        pS = pSs[par]
                        k.op("act", lambda e, pS=pS, pt=pt, par=par: e.activation(
                            out=pt[:, par * 256:(par + 1) * 256], in_=ps[pS][:, 0:256], func=AF.Exp, scale=0.125),
                             reads=[r_ps[pS]], writes=[r_pt])

                def stageB(i):
                    p, tk, prev, cur, pt, r_pt = info.pop(i)
                    pO = i % 2
                    kbs = ([prev] if prev is not None else []) + [cur]
                    for par in range(2):
                        for (col, lh) in ((par * 128, None), (256 + par * 128, self.ones_b)):
                            for j, kb in enumerate(kbs):
                                jj = j if prev is not None else 1
                                lhsT = kb[2] if lh is None else lh[:, :]
                                k.op("pe", lambda e, pO=pO, col=col, lhsT=lhsT, pt=pt, par=par, jj=jj, j=j, n=len(kbs): e.matmul(
                                    ps[pO][:, col:col + 128], lhsT=lhsT, rhs=pt[:, par * 256 + jj * 128:par * 256 + (jj + 1) * 128],
                                    start=(j == 0), stop=(j == n - 1)),
                                     reads=kb[3] + [r_pt, self.r_const], writes=[r_ps[pO]])
                    for par in range(2):
                        rows = slice(64 * par, 64 * par + 64)
                        r_dst = r_acc[0][par]
                        src = ps[pO][rows, :].rearrange("p (q h t) -> p q h t", q=2, h=2)[:, :, par, :]
                        if p == 0:
                            k.op("dve", lambda e, rows=rows, tk=tk, src=src: e.tensor_copy(out=OD[rows, :, tk], in_=src),
                                 reads=[r_ps[pO]], writes=[r_dst])
                        else:
                            k.op("dve", lambda e, rows=rows, tk=tk, src=src: e.tensor_tensor(
                                out=OD[rows, :, tk], in0=OD[rows, :, tk], in1=src, op=ALU.add),
                                 reads=[r_ps[pO], r_dst], writes=[r_dst])

                nblk = len(blocks)
                stageA(0)
                stageA(1)
                for i in range(nblk):
                    if i + 2 < nblk:
                        stageA(i + 2)
                    stageB(i)
                r_all = r_acc[0]
                k.op("act", lambda e: e.activation(out=den, in_=den, func=AF.Ln), reads=r_all, writes=r_all)
                k.op("act", lambda e: e.activation(out=den, in_=den, func=AF.Exp, scale=-1.0), reads=r_all, writes=r_all)
                k.op("dve", lambda e, hp=hp: e.tensor_tensor(out=M.y[:, hp, :], in0=Oacc, in1=den, op=ALU.mult),
                     reads=r_all, writes=[M.r_y[hp]])
                k.op("pool", lambda e, hp=hp: e.tensor_copy(out=M.kTp1[:, hp, :], in_=X[1][0][:, TM - 128:TM]),
                     reads=[r_X[1][0]], writes=[M.r_prev[hp]])
                k.op("pool", lambda e, hp=hp: e.tensor_copy(out=M.kTp2[:, hp, :], in_=X[1][1][:, TM - 512:TM]),
                     reads=[r_X[1][1]], writes=[M.r_prev[hp]])
                k.op("pool", lambda e, hp=hp: e.tensor_copy(out=M.kTp3[:, hp, :], in_=X[1][2][:, :]),
                     reads=[r_X[1][2]], writes=[M.r_prev[hp]])
                k.op("pool", lambda e, hp=hp: e.tensor_copy(out=M.V1p[:, hp, :], in_=Vc[0][:, 15, :]),
                     reads=[r_V[0]], writes=[M.r_prev[hp]])
                k.op("pool", lambda e, hp=hp: e.tensor_copy(out=M.V2p[:, hp, :, :], in_=Vc[1][:, 12:16, :]),
                     reads=[r_V[1]], writes=[M.r_prev[hp]])
                k.op("pool", lambda e, hp=hp: e.tensor_copy(out=M.V3p[:, hp, :, :], in_=Vc[2][:, :, :]),
                     reads=[r_V[2]], writes=[M.r_prev[hp]])
            k.barrier()


def make_consts():
    i = np.arange(128)
    ident = np.eye(128, dtype=np.float32)
    prev = (i[:, None] >= i[None, :]).astype(np.float32)
    cur = (i[:, None] <= i[None, :]).astype(np.float32)
    z = np.zeros((128, 128), np.float32)
    blk = (i[:, None] // 64 == i[None, :] // 64).astype(np.float32)
    rst = np.broadcast_to((np.arange(512) % 64 != 0).astype(np.float32)[None, :], (128, 512))
    return np.ascontiguousarray(np.concatenate([ident, prev, cur, prev, cur, z, cur, z, cur, blk, rst], axis=1))


PK_L = 72


def make_packed(inputs):
    g = lambda n: np.asarray(inputs[n], dtype=np.float32)
    pidx = np.arange(128)
    cols = []
    for l in range(DEPTH):
        cw = g("conv_w")[l]
        cols.append(cw.reshape(4, 8, 128).transpose(2, 0, 1).reshape(128, 32))
        cols.append(g("conv_b")[l].reshape(8, 128).T)
        cols.append(np.broadcast_to(g("a_log")[l][None, :], (128, 8)))
        cols.append(np.broadcast_to(g("dt_bias")[l][None, :], (128, 8)))
        hd = (2 * np.arange(4)[None, :] + (pidx[:, None] // 64))
        cols.append(g("d_skip")[l][hd])
        cols.append(g("ssm_norm")[l].reshape(4, 128).T)
        cols.append(g("hgrn_norm")[l].reshape(2, 128).T)
        cols.append(g("hgrn_lb_logits")[0].reshape(2, 128).T)
        cols.append(g("hgrn_lb_logits")[1].reshape(2, 128).T)
        cols.append(np.zeros((128, PK_L - 70), np.float32))
    return np.ascontiguousarray(np.concatenate(cols, axis=1), dtype=np.float32)


def make_in_map(inputs, b, L=None):
    m = {}
    for n, v in inputs.items():
        v = np.asarray(v)
        if n == "x":
            xs = v[b]
            if L is not None:
                xs = xs[:L]
            m["x"] = np.ascontiguousarray(xs, dtype=np.float32)
        else:
            m[n] = np.ascontiguousarray(v, dtype=np.float32)
    m["cst_d"] = make_consts()
    m["pk_d"] = make_packed(inputs)
    return m


def build_two_pass(L, **kw):
    p1 = Prog(L, **kw)
    p1.build()
    needed = p1.k.needed
    del p1
    p2 = Prog(L, needed_in=needed, **kw)
    p2.build()
    return p2


def kernel(**inputs):
    x = np.asarray(inputs["x"])
    B, L, _ = x.shape
    prog = build_two_pass(L)
    nc = prog.nc
    in_maps = [make_in_map(inputs, b) for b in range(B)]
    res = run_bass_kernel_spmd(nc, in_maps, core_ids=list(range(B)))
    return np.stack([np.asarray(r["out"], dtype=np.float32) for r in res.results], axis=0)
```

```python
import numpy as np
from contextlib import ExitStack
import concourse.bass as bass
import concourse.mybir as mybir
from concourse.bass_utils import run_bass_kernel_spmd

F32 = mybir.dt.float32
BF16 = mybir.dt.bfloat16
ALU = mybir.AluOpType
AF = mybir.ActivationFunctionType

D = 1024
DFF = 2816
NFC = DFF // 128
DEPTH = 2
DIN = 3336
EPS = 1e-6


class TL:
    def __init__(self, name, sem, step):
        self.name, self.sem, self.step, self.val = name, sem, step, 0


class Res:
    __slots__ = ("name", "w", "r", "excl")

    def __init__(self, name, excl=False):
        self.name = name
        self.w = None
        self.r = {}
        self.excl = excl


class KB:
    def __init__(self, nc, es):
        self.nc = nc
        self.es = es
        self.eng = {"pe": nc.tensor, "act": nc.scalar, "dve": nc.vector, "pool": nc.gpsimd, "sp": nc.sync}
        self.tl = {}
        for e in ("pe", "act", "dve", "pool"):
            self.tl[e] = TL(e, es.enter_context(nc.semaphore("sem_" + e)), 1)
        self.known = {e: {} for e in self.eng}
        self.all_tls = list(self.tl.values())
        self.nres = 0
        self.ninst = {e: 0 for e in self.eng}
        self.needed = {e: set() for e in self.tl}
        self.needed_in = None
        self.phys = {e: 0 for e in self.tl}
        self.l2p = {e: {} for e in self.tl}
        self.tl_eng = {id(t): e for e, t in self.tl.items()}

    def res(self, name=None, excl=False):
        self.nres += 1
        return Res(name or f"r{self.nres}", excl)

    def dma_tl(self, name):
        t = TL(name, self.es.enter_context(self.nc.semaphore("dsem_" + name)), 16)
        self.all_tls.append(t)
        return t

    def _wait(self, eng, deps):
        kn = self.known[eng]
        own = self.tl.get(eng)
        need = {}
        for (tl, val) in deps:
            if tl is own:
                if eng == "pe":
                    continue
                if own.val - val >= 3:
                    continue
            if kn.get(tl, 0) >= val:
                continue
            if need.get(tl, 0) < val:
                need[tl] = val
        for tl, val in need.items():
            self._emit_wait(eng, tl, val)
            kn[tl] = val

    def _emit_wait(self, eng, tl, val):
        te = self.tl_eng.get(id(tl))
        pv = val
        if te is not None:
            self.needed[te].add(val)
            if self.needed_in is not None:
                pv = self.l2p[te][val]
        self.eng[eng].wait_ge(tl.sem, pv)
        self.ninst[eng] += 1

    def _deps(self, reads, writes):
        deps = []
        for r in reads:
            if r.w is not None:
                deps.append(r.w)
            if r.excl:
                deps.extend(r.r.items())
        for w in writes:
            if w.w is not None:
                deps.append(w.w)
            deps.extend(w.r.items())
        return deps

    def _stamp(self, tl, reads, writes):
        for r in reads:
            if r.r.get(tl, 0) < tl.val:
                r.r[tl] = tl.val
        for w in writes:
            w.w = (tl, tl.val)
            w.r = {}

    def op(self, eng, fn, reads=(), writes=()):
        self._wait(eng, self._deps(reads, writes))
        inst = fn(self.eng[eng])
        tl = self.tl[eng]
        tl.val += 1
        if self.needed_in is None or tl.val in self.needed_in[eng]:
            inst.then_inc(tl.sem, 1)
            self.phys[eng] += 1
            self.l2p[eng][tl.val] = self.phys[eng]
        self.ninst[eng] += 1
        self._stamp(tl, reads, writes)
        return inst

    def dma(self, eng, tl, out, in_, reads=(), writes=(), **kw):
        deps = self._deps(reads, writes)
        if tl.val > 0:
            deps.append((tl, tl.val))
        self._wait(eng, deps)
        inst = self.eng[eng].dma_start(out=out, in_=in_, **kw)
        tl.val += 16
        inst.then_inc(tl.sem, 16)
        self.ninst[eng] += 1
        self._stamp(tl, reads, writes)
        return inst

    def barrier(self, engs=("pe", "act", "dve", "pool", "sp")):
        for e in engs:
            self._wait(e, [(t, t.val) for t in self.all_tls if t.val > 0 and t is not self.tl.get(e)])
            own = self.tl.get(e)
            if own is not None and own.val > 0 and e != "pe":
                if self.known[e].get(own, 0) < own.val:
                    self._emit_wait(e, own, own.val)
                    self.known[e][own] = own.val


class Prog:
    def __init__(self, L, n_layers=DEPTH, do_mixer=True, mix_parts=("attn", "ssd", "hgrn"), needed_in=None):
        self.L = L
        self.n_layers = n_layers
        self.do_mixer = do_mixer
        self.mix_parts = mix_parts
        self.nc = bass.Bass("TRN2", target_bir_lowering=False)
        self.es = ExitStack()
        self.k = KB(self.nc, self.es)
        self.k.needed_in = needed_in

    def sb(self, es, name, shape, dt):
        self._uid = getattr(self, "_uid", 0) + 1
        return es.enter_context(self.nc.sbuf_tensor(f"{name}_{self._uid}", shape, dt))

    def pst(self, es, name, shape, dt):
        self._uid = getattr(self, "_uid", 0) + 1
        return es.enter_context(self.nc.psum_tensor(f"{name}_{self._uid}", shape, dt))

    def declare(self):
        nc, L = self.nc, self.L
        di = lambda n, s: nc.dram_tensor(n, s, F32, kind="ExternalInput").ap()
        self.x = di("x", [L, D])
        self.inp = {}
        shapes = {
            "ffn1_norm": [DEPTH, D], "ffn1_w_gate": [DEPTH, D, DFF], "ffn1_w_up": [DEPTH, D, DFF],
            "ffn1_w_down": [DEPTH, DFF, D], "mix_norm": [DEPTH, D], "w_in": [DEPTH, D, DIN],
            "conv_w": [DEPTH, 4, 1024], "conv_b": [DEPTH, 1024], "dt_bias": [DEPTH, 8], "a_log": [DEPTH, 8],
            "d_skip": [DEPTH, 8], "ssm_norm": [DEPTH, 512], "hgrn_lb_logits": [DEPTH, 256],
            "hgrn_norm": [DEPTH, 256], "w_out": [DEPTH, D, D], "ffn2_norm": [DEPTH, D],
            "ffn2_w_gate": [DEPTH, D, DFF], "ffn2_w_up": [DEPTH, D, DFF], "ffn2_w_down": [DEPTH, DFF, D],
            "final_norm": [D],
        }
        for n, s in shapes.items():
            self.inp[n] = di(n, s)
        self.out = nc.dram_tensor("out", [L, D], F32, kind="ExternalOutput").ap()
        self.hbuf = nc.dram_tensor("hbuf", [128, 8, L], F32, kind="Internal").ap()
        self.win = [nc.dram_tensor(f"win_{l}", [128, 8, DIN], BF16, kind="Internal").ap() for l in range(self.n_layers)]
        self.wout = [nc.dram_tensor(f"wout_{l}", [128, 8, D], BF16, kind="Internal").ap() for l in range(self.n_layers)]
        self.wgu = {}
        self.wd = {}
        for l in range(self.n_layers):
            for f in (1, 2):
                self.wgu[(l, f)] = nc.dram_tensor(f"wgu_{l}_{f}", [11, 128, 2, 8, 256], BF16, kind="Internal").ap()
                self.wd[(l, f)] = nc.dram_tensor(f"wd_{l}_{f}", [8, 128, NFC, 128], BF16, kind="Internal").ap()

    def convert_ffn(self, l, f):
        k = self.k
        wg = self.inp[f"ffn{f}_w_gate"][l].rearrange("(kc p) c -> p kc c", p=128)
        wu = self.inp[f"ffn{f}_w_up"][l].rearrange("(kc p) c -> p kc c", p=128)
        wdn = self.inp[f"ffn{f}_w_down"][l].rearrange("(fc p) c -> p fc c", p=128)
        self.cv_g = getattr(self, "cv_g", {})
        self.cv_d = getattr(self, "cv_d", {})
        for gi, grp in enumerate(((0, 1), (2, 3, 4), (5, 6, 7), (8, 9, 10))):
            tl = k.dma_tl(f"cvg{l}{f}{gi}")
            r = k.res(f"cvg{l}{f}{gi}")
            for g in grp:
                self._cv(tl, r, self.wgu[(l, f)][g, :, 0, :, :], wg[:, :, g * 256:(g + 1) * 256])
                self._cv(tl, r, self.wgu[(l, f)][g, :, 1, :, :], wu[:, :, g * 256:(g + 1) * 256])
                self.cv_g[(l, f, g)] = r
        for gi, grp in enumerate(((0, 1, 2, 3), (4, 5, 6, 7))):
            tl = k.dma_tl(f"cvd{l}{f}{gi}")
            r = k.res(f"cvd{l}{f}{gi}")
            for dc in grp:
                self._cv(tl, r, self.wd[(l, f)][dc, :, :, :], wdn[:, :, dc * 128:(dc + 1) * 128])
                self.cv_d[(l, f, dc)] = r

    def convert_mix(self, l):
        k = self.k
        if not hasattr(self, "mixcv_res"):
            self.mixcv_res = {}
        tl = k.dma_tl(f"cvm{l}")
        r = k.res(f"cvmres{l}")
        self.mixcv_res[l] = r
        wi = self.inp["w_in"][l].rearrange("(kc p) c -> p kc c", p=128)
        wo = self.inp["w_out"][l].rearrange("(kc p) c -> p kc c", p=128)
        for kc in range(8):
            self._cv(tl, r, self.win[l][:, kc, :], wi[:, kc, :])
            self._cv(tl, r, self.wout[l][:, kc, :], wo[:, kc, :])

    def convert_layer(self, l):
        self.convert_ffn(l, 1)
        if self.do_mixer:
            self.convert_mix(l)
        self.convert_ffn(l, 2)

    def _cv(self, tl, r, out, in_):
        k = self.k
        inst = k.eng["pool"].dma_start(out=out, in_=in_)
        tl.val += 16
        inst.then_inc(tl.sem, 16)
        r.w = (tl, tl.val)

    def setup_consts(self):
        k, nc, es = self.k, self.nc, self.es
        self.ones_b = self.sb(es, "ones_b", [128, 128], BF16)
        self.r_const = k.res("consts")
        self.cst_d = nc.dram_tensor("cst_d", [128, 128 + 1024 + 128 + 512], F32, kind="ExternalInput").ap()
        self.pk_d = nc.dram_tensor("pk_d", [128, DEPTH * PK_L], F32, kind="ExternalInput").ap()
        self.pk = self.sb(es, "pk", [128, DEPTH * PK_L], F32)
        self.tri_b = self.sb(es, "tri_b", [128, 128], BF16)
        tl = k.dma_tl("const")
        self.const_tl = tl
        self.cst_f = self.sb(es, "cst_f", [128, 128 + 1024 + 128 + 512], F32)
        self.rst_f = self.cst_f[:, 1280:1792]
        self.blk_b = self.sb(es, "blk_b", [128, 128], BF16)
        self.ident_f = self.cst_f[:, 0:128]
        self.tri_f = self.cst_f[:, 256:384]
        self.mask_b = self.sb(es, "mask_b", [128, 1024], BF16)
        self.ones_f = self.sb(es, "ones_f", [128, 128], F32)
        self.ident_b = self.sb(es, "ident_b", [128, 128], BF16)
        k.dma("sp", tl, self.cst_f[:], self.cst_d[:, :], writes=[self.r_const])
        k.dma("sp", tl, self.pk[:], self.pk_d[:, :], writes=[self.r_const])
        k.op("dve", lambda e: e.tensor_copy(out=self.blk_b[:], in_=self.cst_f[:, 1152:1280]),
             reads=[self.r_const], writes=[self.r_const])
        k.op("dve", lambda e: e.tensor_copy(out=self.tri_b[:], in_=self.cst_f[:, 256:384]),
             reads=[self.r_const], writes=[self.r_const])
        k.op("dve", lambda e: e.memset(self.ones_b[:], 1.0), writes=[self.r_const])
        k.op("dve", lambda e: e.memset(self.ones_f[:], 1.0), writes=[self.r_const])
        k.op("dve", lambda e: e.tensor_copy(out=self.mask_b[:], in_=self.cst_f[:, 128:1152]),
             reads=[self.r_const], writes=[self.r_const])
        self.mbias_b = self.sb(es, "mbias_b", [128, 1024], BF16)
        k.op("dve", lambda e: e.tensor_scalar(out=self.mbias_b[:], in0=self.cst_f[:, 128:1152], scalar1=-1.0, scalar2=30000.0,
                                              op0=ALU.add, op1=ALU.mult), reads=[self.r_const], writes=[self.r_const])
        k.op("dve", lambda e: e.tensor_copy(out=self.ident_b[:], in_=self.cst_f[:, 0:128]),
             reads=[self.r_const], writes=[self.r_const])
        self.normw = {}
        names = []
        for l in range(self.n_layers):
            names += [("ffn1_norm", l), ("mix_norm", l), ("ffn2_norm", l)]
        names.append(("final_norm", None))
        self.normw_sb = self.sb(es, "normw", [128, len(names), 8], F32)
        for i, (n, l) in enumerate(names):
            src = self.inp[n][l] if l is not None else self.inp[n]
            k.dma("sp", tl, self.normw_sb[:, i, :], src.rearrange("(c p) -> p c", p=128),
                  writes=[self.r_const], allow_slow_non_contiguous=True)
            self.normw[(n, l)] = i
        k.op("dve", lambda e: e.tensor_scalar(out=self.normw_sb[:], in0=self.normw_sb[:], scalar1=float(np.sqrt(D)),
                                              scalar2=None, op0=ALU.mult),
             reads=[self.r_const], writes=[self.r_const])

    def rmsnorm(self, h_ap, r_h, nw_idx, out_fn, r_out, sq, r_sq, ps, r_ps, rstd, r_rstd, ntok):
        k = self.k
        k.op("act", lambda e: e.activation(out=sq[:, :, :ntok], in_=h_ap, func=AF.Square),
             reads=[r_h], writes=[r_sq])
        for c in range(8):
            k.op("pe", lambda e, c=c: e.matmul(ps[:, :ntok], lhsT=self.ones_b[:, :], rhs=sq[:, c, :ntok],
                                               start=(c == 0), stop=(c == 7)),
                 reads=[r_sq, self.r_const], writes=[r_ps])
        k.op("act", lambda e: e.activation(out=rstd[:, :ntok], in_=ps[:, :ntok], func=AF.Sqrt,
                                           bias=float(D * EPS), scale=1.0),
             reads=[r_ps], writes=[r_rstd])
        k.op("dve", lambda e: e.reciprocal(out=rstd[:, :ntok], in_=rstd[:, :ntok]),
             reads=[r_rstd], writes=[r_rstd])
        for c in range(8):
            k.op("dve", lambda e, c=c: e.scalar_tensor_tensor(
                out=out_fn(c), in0=h_ap[:, c, :], scalar=self.normw_sb[:, nw_idx, c:c + 1], in1=rstd[:, :ntok],
                op0=ALU.mult, op1=ALU.mult),
                 reads=[r_h, r_rstd, self.r_const], writes=[r_out])

    def ffn_phase(self, l, f, first, last):
        k, nc, L = self.k, self.nc, self.L
        TF = 1024 if L >= 1024 else L
        NS = TF // 512
        ntiles = L // TF
        with ExitStack() as es:
            h_sb = [self.sb(es, f"h{i}", [128, 8, TF], F32) for i in range(2)]
            r_h = [k.res(f"h{i}") for i in range(2)]
            tl_h = [k.dma_tl(f"h{l}{f}{i}") for i in range(2)]
            tl_st = [k.dma_tl(f"st{l}{f}{i}") for i in range(2)]
            xn2 = [self.sb(es, f"xn{i}", [128, 8, TF], BF16) for i in range(2)]
            r_xn2 = [[k.res(f"xn{i}_{s}") for s in range(NS)] for i in range(2)]
            act = self.sb(es, "act", [128, NFC, TF], BF16)
            r_act = [[k.res() for s in range(NS)] for c in range(NFC)]
            NW = 2
            wgu = [self.sb(es, f"wgu{i}", [128, 2, 8, 256], BF16) for i in range(NW)]
            r_wgu = [k.res(f"wgu{i}") for i in range(NW)]
            tl_wgu = [k.dma_tl(f"wgu{l}{f}{i}") for i in range(NW)]
            wd = [self.sb(es, f"wd{i}", [128, NFC, 128], BF16) for i in range(2)]
            r_wd = [k.res(f"wd{i}") for i in range(2)]
            tl_wd = [k.dma_tl(f"wd{l}{f}{i}") for i in range(2)]
            sq = self.sb(es, "sq", [128, 8, 512], BF16)
            r_sq = k.res("sq")
            rstd = self.sb(es, "rstd", [128, 512], F32)
            r_rstd = k.res("rstd")
            sg = [self.sb(es, f"sg{i}", [128, 512], F32) for i in range(2)]
            r_sg = [k.res() for i in range(2)]
            if first:
                xtok = [self.sb(es, f"xtok{i}", [128, D], F32) for i in range(2)]
                r_xtok = [k.res() for i in range(2)]
                tl_xtok = [k.dma_tl(f"xtok{i}") for i in range(2)]
            if last:
                otok = [self.sb(es, f"otok{i}", [128, D], F32) for i in range(2)]
                r_otok = [k.res() for i in range(2)]
                tl_otok = [k.dma_tl(f"otok{i}") for i in range(2)]
            ps = [self.pst(es, f"ps{i}", [128, 512], F32) for i in range(8)]
            r_ps = [k.res(f"ps{i}", excl=True) for i in range(8)]
            nwi = self.normw[(f"ffn{f}_norm", l)]
            wgu_d, wd_d = self.wgu[(l, f)], self.wd[(l, f)]
            r_hb = self.r_hbuf

            wq = {"g": 0, "d": 0}

            def load_tile(t):
                b = t % 2
                if not first:
                    k.dma("sp", tl_h[b], h_sb[b][:], self.hbuf[:, :, t * TF:(t + 1) * TF],
                          reads=[r_hb[t]], writes=[r_h[b]])
                else:
                    for blk in range(TF // 128):
                        xb = blk % 2
                        t0 = t * TF + blk * 128
                        k.dma("sp", tl_xtok[xb], xtok[xb][:], self.x[t0:t0 + 128, :], writes=[r_xtok[xb]])
                        for half in range(2):
                            pi = 0 if half == 0 else 7
                            for c4 in range(4):
                                c = half * 4 + c4
                                k.op("pe", lambda e, c=c, c4=c4, pi=pi, xb=xb: e.transpose(
                                    out=ps[pi][:, c4 * 128:(c4 + 1) * 128], in_=xtok[xb][:, c * 128:(c + 1) * 128],
                                    identity=self.ident_f),
                                     reads=[r_xtok[xb], self.r_const], writes=[r_ps[pi]])
                            k.op("act" if half == 0 else "dve", lambda e, half=half, pi=pi, blk=blk, b=b: (
                                e.activation(out=h_sb[b][:, half * 4:(half + 1) * 4, blk * 128:(blk + 1) * 128],
                                             in_=ps[pi][:, :].rearrange("p (c t) -> p c t", c=4), func=AF.Copy)
                                if half == 0 else
                                e.tensor_copy(out=h_sb[b][:, half * 4:(half + 1) * 4, blk * 128:(blk + 1) * 128],
                                              in_=ps[pi][:, :].rearrange("p (c t) -> p c t", c=4))),
                                 reads=[r_ps[pi]], writes=[r_h[b]])

            def norm_tile(t):
                b = t % 2
                for s in range(NS):
                    sl = slice(s * 512, (s + 1) * 512)
                    self.rmsnorm(h_sb[b][:, :, sl], r_h[b], nwi, lambda c, sl=sl, b=b: xn2[b][:, c, sl], r_xn2[b][s],
                                 sq, r_sq, ps[0], r_ps[0], rstd, r_rstd, 512)

            load_tile(0)
            norm_tile(0)
            for t in range(ntiles):
                b = t % 2
                xn, r_xn = xn2[b], r_xn2[b]
                if t + 1 < ntiles and not first:
                    load_tile(t + 1)
                for g in range(11):
                    wslot = wq["g"] % NW
                    wq["g"] += 1
                    k.dma("sp", tl_wgu[wslot], wgu[wslot][:], wgu_d[g], reads=[self.cv_g[(l, f, g)]], writes=[r_wgu[wslot]])
                    for c2 in range(2):
                        fc = g * 2 + c2
                        for s in range(NS):
                            sl = slice(s * 512, (s + 1) * 512)
                            pg, pu = 1 + (fc * NS + s) % 2, 3 + (fc * NS + s) % 2
                            for (pp, gi) in ((pg, 0), (pu, 1)):
                                for kc in range(8):
                                    k.op("pe", lambda e, pp=pp, gi=gi, kc=kc, c2=c2, sl=sl, wslot=wslot: e.matmul(
                                        ps[pp][:, :], lhsT=wgu[wslot][:, gi, kc, c2 * 128:(c2 + 1) * 128],
                                        rhs=xn[:, kc, sl], start=(kc == 0), stop=(kc == 7)),
                                         reads=[r_wgu[wslot], r_xn[s]], writes=[r_ps[pp]])
                            si = (fc * NS + s) % 2
                            k.op("act", lambda e, si=si, pg=pg: e.activation(out=sg[si][:], in_=ps[pg][:, :], func=AF.Silu),
                                 reads=[r_ps[pg]], writes=[r_sg[si]])
                            k.op("dve", lambda e, si=si, pu=pu, fc=fc, sl=sl: e.tensor_tensor(
                                out=act[:, fc, sl], in0=sg[si][:], in1=ps[pu][:, :], op=ALU.mult),
                                 reads=[r_sg[si], r_ps[pu]], writes=[r_act[fc][s]])
                if t + 1 < ntiles:
                    if first:
                        load_tile(t + 1)
                    norm_tile(t + 1)
                for dc in range(8):
                    wslot = wq["d"] % 2
                    wq["d"] += 1
                    k.dma("sp", tl_wd[wslot], wd[wslot][:], wd_d[dc], reads=[self.cv_d[(l, f, dc)]], writes=[r_wd[wslot]])
                    for s in range(NS):
                        sl = slice(s * 512, (s + 1) * 512)
                        pd = 5 + (dc * NS + s) % 2
                        for fc in range(NFC):
                            k.op("pe", lambda e, pd=pd, fc=fc, sl=sl, wslot=wslot: e.matmul(
                                ps[pd][:, :], lhsT=wd[wslot][:, fc, :], rhs=act[:, fc, sl],
                                start=(fc == 0), stop=(fc == NFC - 1)),
                                 reads=[r_wd[wslot], r_act[fc][s]], writes=[r_ps[pd]])
                        k.op("dve", lambda e, pd=pd, dc=dc, sl=sl, b=b: e.scalar_tensor_tensor(
                            out=h_sb[b][:, dc, sl], in0=ps[pd][:, :], scalar=0.5, in1=h_sb[b][:, dc, sl],
                            op0=ALU.mult, op1=ALU.add),
                             reads=[r_ps[pd], r_h[b]], writes=[r_h[b]])
                if not last:
                    k.dma("sp", tl_st[b], self.hbuf[:, :, t * TF:(t + 1) * TF], h_sb[b][:],
                          reads=[r_h[b]], writes=[r_hb[t]])
                else:
                    fwi = self.normw[("final_norm", None)]
                    for s in range(NS):
                        sl = slice(s * 512, (s + 1) * 512)
                        ofm = h_sb[b][:, :, sl]
                        r_ofm = r_h[b]
                        self.rmsnorm(h_sb[b][:, :, sl], r_h[b], fwi, lambda c, ofm=ofm: ofm[:, c, :], r_ofm,
                                     sq, r_sq, ps[0], r_ps[0], rstd, r_rstd, 512)
                        for blk in range(4):
                            ob = blk % 2
                            t0 = t * TF + s * 512 + blk * 128
                            for half in range(2):
                                pi = 1 + half
                                for c4 in range(4):
                                    c = half * 4 + c4
                                    k.op("pe", lambda e, c=c, c4=c4, pi=pi, blk=blk: e.transpose(
                                        out=ps[pi][:, c4 * 128:(c4 + 1) * 128], in_=ofm[:, c, blk * 128:(blk + 1) * 128],
                                        identity=self.ident_f),
                                         reads=[r_ofm, self.r_const], writes=[r_ps[pi]])
                                if half == 0:
                                    k.op("act", lambda e, ob=ob, pi=pi: e.activation(
                                        out=otok[ob][:, 0:512], in_=ps[pi][:, :], func=AF.Copy),
                                         reads=[r_ps[pi]], writes=[r_otok[ob]])
                                else:
                                    k.op("dve", lambda e, ob=ob, pi=pi: e.tensor_copy(
                                        out=otok[ob][:, 512:1024], in_=ps[pi][:, :]),
                                         reads=[r_ps[pi]], writes=[r_otok[ob]])
                            k.dma("sp", tl_otok[ob], self.out[t0:t0 + 128, :], otok[ob][:],
                                  reads=[r_otok[ob]], writes=[self.r_out])
            k.barrier()

    def build(self):
        k = self.k
        self.declare()
        TFm = 1024 if self.L >= 1024 else self.L
        self.r_hbuf = [k.res(f"hb{t}") for t in range(self.L // TFm)]
        self.r_out = k.res("out")
        self.setup_consts()
        nl = self.n_layers
        self.convert_layer(0)
        for l in range(nl):
            self.ffn_phase(l, 1, first=(l == 0), last=False)
            if self.do_mixer:
                self.mixer_phase(l)
            if l + 1 < nl:
                self.convert_layer(l + 1)
            self.ffn_phase(l, 2, first=False, last=(l == nl - 1))
        k.barrier()
        self.es.close()
        return self.nc

    def mixer_phase(self, l):
        k, nc, L = self.k, self.nc, self.L
        TM = 2048
        assert L % TM == 0
        ntiles = L // TM
        with ExitStack() as es:
            M = type("M", (), {})()
            self.M = M
            M.l, M.TM, M.es = l, TM, es
            M.u = self.sb(es, "u", [128, 8, TM], BF16)
            M.r_u = [k.res(f"u{s}") for s in range(4)]
            M.y = self.sb(es, "y", [128, 8, TM], BF16)
            M.r_y = [k.res(f"y{c}") for c in range(8)]
            M.tl_hp = [k.dma_tl(f"mhp{l}{i}") for i in range(2)]
            M.tl_hst = [k.dma_tl(f"mhst{l}{i}") for i in range(2)]
            M.tl_hq = [k.dma_tl(f"mhq{l}{i}") for i in range(2)]
            M.wsl = [self.sb(es, f"wsl{i}", [128, 8, 256], BF16) for i in range(2)]
            M.r_wsl = [k.res(f"wsl{i}") for i in range(2)]
            M.tl_wsl = [k.dma_tl(f"mw{l}{i}") for i in range(2)]
            M.wcnt = 0
            M.wsm = [self.sb(es, f"wsm{i}", [128, 8, 128], BF16) for i in range(3)]
            M.r_wsm = [k.res(f"wsm{i}") for i in range(3)]
            M.tl_wsm = [k.dma_tl(f"mwsm{l}{i}") for i in range(3)]
            M.ps = [self.pst(es, f"mps{i}", [128, 512], F32) for i in range(8)]
            M.r_ps = [k.res(f"mps{i}", excl=True) for i in range(8)]
            M.kTp1 = self.sb(es, "kTp1", [128, 2, 128], BF16)
            M.kTp2 = self.sb(es, "kTp2", [128, 2, 512], BF16)
            M.kTp3 = self.sb(es, "kTp3", [128, 2, TM], BF16)
            M.V1p = self.sb(es, "V1p", [128, 2, 128], BF16)
            M.V2p = self.sb(es, "V2p", [128, 2, 4, 128], BF16)
            M.V3p = self.sb(es, "V3p", [128, 2, 16, 128], BF16)
            M.r_prev = [k.res(f"aprev{i}") for i in range(2)]
            M.S = self.sb(es, "ssdS", [128, 512], F32)
            M.S_bf = self.sb(es, "ssdSb", [128, 512], BF16)
            M.ctail = self.sb(es, "ctail", [128, 8, 3], F32)
            M.wdt = self.sb(es, "wdt", [128, 8, 8], BF16)
            M.A_b = self.sb(es, "A_b", [128, 8], F32)
            M.r_S, M.r_Sbf, M.r_ctail, M.r_ssdc = k.res("S"), k.res("Sbf"), k.res("ctail"), k.res("ssdc")
            M.tl_misc = k.dma_tl(f"mmisc{l}")
            po = l * PK_L
            M.po = po
            k.op("pool", lambda e: e.memset(M.S[:], 0.0), writes=[M.r_S])
            k.op("pool", lambda e: e.memset(M.S_bf[:], 0.0), writes=[M.r_Sbf])
            k.op("pool", lambda e: e.memset(M.ctail[:], 0.0), writes=[M.r_ctail])
            k.dma("sp", M.tl_misc, M.wdt[:], self.win[l][:, :, 2304:2312], reads=[self.mixcv_res[l]], writes=[M.r_ssdc])
            k.op("act", lambda e: e.activation(out=M.A_b[:], in_=self.pk[:, po + 40:po + 48], func=AF.Exp),
                 reads=[self.r_const], writes=[M.r_ssdc])
            k.op("dve", lambda e: e.tensor_scalar(out=M.A_b[:], in0=M.A_b[:], scalar1=-1.0, scalar2=None, op0=ALU.mult),
                 reads=[M.r_ssdc], writes=[M.r_ssdc])
            M.diagW = self.sb(es, "diagW", [128, 4, 8, 128], BF16)
            for j_ in range(4):
                for c_ in range(8):
                    k.op("dve", lambda e, j_=j_, c_=c_: e.tensor_scalar(
                        out=M.diagW[:, j_, c_, :], in0=self.ident_f, scalar1=self.pk[:, po + j_ * 8 + c_:po + j_ * 8 + c_ + 1],
                        scalar2=None, op0=ALU.mult), reads=[self.r_const], writes=[M.r_ssdc])
            M.diagD = self.sb(es, "diagD", [128, 4, 128], BF16)
            for hp_ in range(4):
                k.op("dve", lambda e, hp_=hp_: e.tensor_scalar(
                    out=M.diagD[:, hp_, :], in0=self.ident_f, scalar1=self.pk[:, po + 56 + hp_:po + 57 + hp_], scalar2=None,
                    op0=ALU.mult), reads=[self.r_const], writes=[M.r_ssdc])
            M.HS = self.sb(es, "hgS", [128, 2, 128], F32)
            M.HS_bf = self.sb(es, "hgSb", [128, 2, 128], BF16)
            M.lb = self.sb(es, "hglb", [128, 2], F32)
            M.r_HS, M.r_HSb, M.r_lb = k.res("HS"), k.res("HSb"), k.res("lb")
            k.op("pool", lambda e: e.memset(M.HS[:], 0.0), writes=[M.r_HS])
            k.op("pool", lambda e: e.memset(M.HS_bf[:], 0.0), writes=[M.r_HSb])
            if l == 0:
                k.op("pool", lambda e: e.memset(M.lb[:], 1e-20), writes=[M.r_lb])
            else:
                k.op("dve", lambda e: e.tensor_tensor(out=M.lb[:], in0=self.pk[:, po + 66:po + 68], in1=self.pk[:, po + 68:po + 70],
                                                      op=ALU.subtract), reads=[self.r_const], writes=[M.r_lb])
                k.op("act", lambda e: e.activation(out=M.lb[:], in_=M.lb[:], func=AF.Exp), reads=[M.r_lb], writes=[M.r_lb])
                k.op("dve", lambda e: e.tensor_scalar(out=M.lb[:], in0=M.lb[:], scalar1=1.0, scalar2=None, op0=ALU.add),
                     reads=[M.r_lb], writes=[M.r_lb])
                k.op("dve", lambda e: e.reciprocal(out=M.lb[:], in_=M.lb[:]), reads=[M.r_lb], writes=[M.r_lb])
                k.op("dve", lambda e: e.tensor_scalar(out=M.lb[:], in0=M.lb[:], scalar1=1e-20, scalar2=1.0 - 1e-6,
                                                      op0=ALU.max, op1=ALU.min), reads=[M.r_lb], writes=[M.r_lb])
            nwi = self.normw[("mix_norm", l)]
            r_hb = self.r_hbuf
            import os
            DBG = int(os.environ.get("MIXDBG", "9"))
            def norm_gen(t, es2):
                hp = [self.sb(es2, f"hp{i}", [128, 8, 512], F32) for i in range(2)]
                r_hp = [k.res(f"hp{i}") for i in range(2)]
                sq = self.sb(es2, "msq", [128, 8, 512], BF16)
                r_sq = k.res("msq")
                rstd = self.sb(es2, "mrstd", [128, 512], F32)
                r_rstd = k.res("mrstd")

                def ld(s):
                    t0 = t * TM + s * 512
                    k.dma("sp", M.tl_hp[s % 2], hp[s % 2][:], self.hbuf[:, :, t0:t0 + 512],
                          reads=[r_hb[t0 // 1024]], writes=[r_hp[s % 2]])
                ld(0)
                ld(1)
                for s in range(4):
                    sl = slice(s * 512, (s + 1) * 512)
                    self.rmsnorm(hp[s % 2][:, :, :], r_hp[s % 2], nwi, lambda c, sl=sl: M.u[:, c, sl], M.r_u[s],
                                 sq, r_sq, M.ps[s % 2], M.r_ps[s % 2], rstd, r_rstd, 512)
                    if s + 2 < 4:
                        ld(s + 2)
                    yield

            def wout_gen(t, es2):
                hq = [self.sb(es2, f"hq{i}", [128, 8, 512], F32) for i in range(2)]
                r_hq = [k.res(f"hq{i}") for i in range(2)]

                def ld2(s):
                    t0 = t * TM + s * 512
                    k.dma("sp", M.tl_hq[s % 2], hq[s % 2][:], self.hbuf[:, :, t0:t0 + 512],
                          reads=[r_hb[t0 // 1024]], writes=[r_hq[s % 2]])
                ld2(0)
                ld2(1)
                for s in range(4):
                    t0 = t * TM + s * 512
                    sl = slice(s * 512, (s + 1) * 512)
                    hb, r_hbb = hq[s % 2], r_hq[s % 2]
                    for dc2 in range(4):
                        w, r_w = self.mix_wload(self.wout[l][:, :, dc2 * 256:(dc2 + 1) * 256], 256)
                        for d2 in range(2):
                            dc = dc2 * 2 + d2
                            pi = 4 + dc % 4
                            for kc in range(8):
                                k.op("pe", lambda e, pi=pi, kc=kc, d2=d2, sl=sl, w=w: e.matmul(
                                    M.ps[pi][:, :], lhsT=w[:, kc, d2 * 128:(d2 + 1) * 128], rhs=M.y[:, kc, sl],
                                    start=(kc == 0), stop=(kc == 7)),
                                     reads=[r_w, M.r_y[kc]], writes=[M.r_ps[pi]])
                            k.op("dve", lambda e, pi=pi, dc=dc, hb=hb: e.scalar_tensor_tensor(
                                out=hb[:, dc, :], in0=M.ps[pi][:, :], scalar=1.0, in1=hb[:, dc, :],
                                op0=ALU.mult, op1=ALU.add),
                                 reads=[M.r_ps[pi], r_hbb], writes=[r_hbb])
                    k.dma("sp", M.tl_hst[s % 2], self.hbuf[:, :, t0:t0 + 512], hb[:], reads=[r_hbb], writes=[r_hb[t0 // 1024]])
                    if s + 2 < 4:
                        ld2(s + 2)
                    yield

            with ExitStack() as es2:
                for _ in norm_gen(0, es2):
                    pass
                k.barrier()
            for t in range(ntiles):
                if "attn" in self.mix_parts:
                    self.attention(t)
                else:
                    k.op("dve", lambda e: e.memset(M.y[:, 0:2, :], 0.0), writes=M.r_y[0:2])
                if "ssd" in self.mix_parts:
                    self.ssd(t)
                else:
                    k.op("dve", lambda e: e.memset(M.y[:, 2:6, :], 0.0), writes=M.r_y[2:6])
                if "hgrn" in self.mix_parts:
                    self.hgrn(t)
                else:
                    k.op("dve", lambda e: e.memset(M.y[:, 6:8, :], 0.0), writes=M.r_y[6:8])
                with ExitStack() as es2:
                    gw = wout_gen(t, es2)
                    gn = norm_gen(t + 1, es2) if t + 1 < ntiles else None
                    for s in range(4):
                        if gn is not None:
                            next(gn, None)
                        next(gw, None)
                    for _ in gw:
                        pass
                    if gn is not None:
                        for _ in gn:
                            pass
                    k.barrier()
            k.barrier()

    def mix_wload_n(self, src_ap, ncols, i):
        k, M = self.k, self.M
        k.dma("sp", M.tl_wsm[i], M.wsm[i][:, :, 0:ncols], src_ap, reads=[self.mixcv_res[M.l]], writes=[M.r_wsm[i]])
        return M.wsm[i], M.r_wsm[i]

    def mix_wload(self, src_ap, ncols):
        k, M = self.k, self.M
        i = M.wcnt % 2
        M.wcnt += 1
        k.dma("sp", M.tl_wsl[i], M.wsl[i][:, :, 0:ncols], src_ap, reads=[self.mixcv_res[M.l]], writes=[M.r_wsl[i]])
        return M.wsl[i], M.r_wsl[i]

    def ssd(self, t):
        k, M, l = self.k, self.M, self.M.l
        TM, po = M.TM, M.po
        ps, r_ps = M.ps, M.r_ps
        pk = self.pk
        with ExitStack() as es:
            D2 = lambda n, shp, dt: [self.sb(es, f"{n}{i}", shp, dt) for i in range(2)]
            zs = D2("zs", [128, 4, 512], BF16)
            cin = self.sb(es, "cin", [128, 8, 515], BF16)
            xbcs = D2("xbcs", [128, 8, 512], BF16)
            dtr = self.sb(es, "dtr", [128, 4, 8], F32)
            dts = D2("dts", [128, 4, 8], F32)
            a_sb = D2("a_sb", [128, 4, 8], F32)
            TriH2 = D2("TriH", [128, 8, 128], BF16)
            TriL2 = D2("TriL", [128, 8, 128], BF16)
            a_hi = D2("a_hi", [128, 4, 8], BF16)
            a_lo = D2("a_lo", [128, 4, 8], BF16)
            acs = self.sb(es, "acs", [128, 8], F32)
            ndec = self.sb(es, "ndec", [128, 8], F32)
            nacs = self.sb(es, "nacs", [128, 8], F32)
            dec_s = self.sb(es, "dec_s", [128, 8], F32)
            cdec = D2("cdec", [128, 8], F32)
            seg = self.sb(es, "seg", [128, 8, 128], F32)
            LT = self.sb(es, "LT", [128, 8, 128], BF16)
            Eacs = self.sb(es, "Eacs", [128, 8, 128], F32)
            CsT = D2("CsT", [128, 8, 128], BF16)
            CBm = self.sb(es, "CBm", [128, 2, 128], BF16)
            MT = D2("MT", [128, 8, 128], BF16)
            xB = D2("xB", [128, 768], BF16)
            xdt = D2("xdt", [128, 8, 64], BF16)
            xdec = D2("xdec", [128, 8, 64], BF16)
            ysb1 = self.sb(es, "ysb", [128, 4, 512], F32)
            ysb = [ysb1, ysb1]
            tmpS = self.sb(es, "tmpS", [128, 512], F32)
            ysq = self.sb(es, "ysq", [128, 4, 512], BF16)[:, :, :]
            rs = seg[:, :, :].rearrange("p (g a) t -> p g (a t)", g=2)
            R = {}
            for n in ("cin", "dtr", "acs", "nacs", "ndec", "dec_s", "seg", "LT", "Eacs", "CBm", "tmpS"):
                R[n] = k.res(n)
            R["ysq"], R["rs"] = k.res("ysq"), R["seg"]
            for n in ("TriH", "TriL", "zs", "xbcs", "dts", "a_sb", "a_hi", "a_lo", "cdec", "CsT", "MT", "xB", "xdt", "xdec"):
                R[n] = [k.res(n + "0"), k.res(n + "1")]
            r_ysb1 = k.res("ysb")
            R["ysb"] = [r_ysb1, r_ysb1]
            k.op("pool", lambda e: e.tensor_copy(out=cin[:, :, 0:3], in_=M.ctail[:, :, :]), reads=[M.r_ctail], writes=[R["cin"]])

            def prologue(s):
                sb_ = s % 2
                sl = slice(s * 512, (s + 1) * 512)
                for c2 in range(6):
                    w, r_w = self.mix_wload(self.win[l][:, :, 768 + c2 * 256:768 + (c2 + 1) * 256], 256)
                    for d2 in range(2):
                        c = c2 * 2 + d2
                        pi = 2 + c % 2
                        if c > 0:
                            yield
                        for kc in range(8):
                            k.op("pe", lambda e, pi=pi, kc=kc, d2=d2, w=w: e.matmul(
                                ps[pi][:, :], lhsT=w[:, kc, d2 * 128:(d2 + 1) * 128], rhs=M.u[:, kc, sl],
                                start=(kc == 0), stop=(kc == 7)),
                                 reads=[r_w, M.r_u[s]], writes=[r_ps[pi]])
                        if c < 4:
                            k.op("act", lambda e, pi=pi, c=c: e.activation(out=zs[sb_][:, c, :], in_=ps[pi][:, :], func=AF.Silu),
                                 reads=[r_ps[pi]], writes=[R["zs"][sb_]])
                        else:
                            k.op("act", lambda e, pi=pi, c=c: e.activation(out=cin[:, c - 4, 3:515], in_=ps[pi][:, :], func=AF.Copy),
                                 reads=[r_ps[pi]], writes=[R["cin"]])
                for c in range(8):
                    yield
                    pi = 2 + c % 2
                    for j in range(4):
                        k.op("pe", lambda e, c=c, j=j, pi=pi: e.matmul(
                            ps[pi][:, :], lhsT=M.diagW[:, j, c, :], rhs=cin[:, c, j:j + 512], start=(j == 0), stop=(j == 3)),
                             reads=[R["cin"], M.r_ssdc], writes=[r_ps[pi]])
                    k.op("act", lambda e, c=c, pi=pi: e.activation(
                        out=xbcs[sb_][:, c, :], in_=ps[pi][:, :], func=AF.Silu, bias=pk[:, po + 32 + c:po + 33 + c], scale=1.0),
                         reads=[r_ps[pi], self.r_const], writes=[R["xbcs"][sb_]])
                k.op("pool", lambda e: e.tensor_copy(out=cin[:, :, 0:3], in_=cin[:, :, 512:515]),
                     reads=[R["cin"]], writes=[R["cin"]])
                yield
                for j in range(4):
                    tk = slice(s * 512 + j * 128, s * 512 + (j + 1) * 128)
                    for kc in range(8):
                        k.op("pe", lambda e, j=j, kc=kc, tk=tk: e.matmul(
                            ps[7][:, 400 + j * 8:400 + (j + 1) * 8], lhsT=M.u[:, kc, tk], rhs=M.wdt[:, kc, :],
                            start=(kc == 0), stop=(kc == 7)),
                             reads=[M.r_ssdc, M.r_u[s]], writes=[r_ps[7]])
                k.op("dve", lambda e: e.tensor_tensor(
                    out=dtr[:], in0=pk[:, None, po + 48:po + 56].broadcast_to([128, 4, 8]),
                    in1=ps[7][:, 400:432].rearrange("p (j e) -> p j e", e=8), op=ALU.add),
                     reads=[r_ps[7], self.r_const], writes=[R["dtr"]])
                k.op("act", lambda e: e.activation(out=dtr[:], in_=dtr[:], func=AF.Exp), reads=[R["dtr"]], writes=[R["dtr"]])
                k.op("act", lambda e: e.activation(out=dts[sb_][:], in_=dtr[:], func=AF.Ln, bias=1.0, scale=1.0),
                     reads=[R["dtr"]], writes=[R["dts"][sb_]])
                k.op("dve", lambda e: e.tensor_tensor(
                    out=a_sb[sb_][:], in0=dts[sb_][:], in1=M.A_b[:, None, :].broadcast_to([128, 4, 8]), op=ALU.mult),
                     reads=[R["dts"][sb_], M.r_ssdc], writes=[R["a_sb"][sb_]])
                k.op("dve", lambda e: e.tensor_copy(out=a_hi[sb_][:], in_=a_sb[sb_][:]),
                     reads=[R["a_sb"][sb_]], writes=[R["a_hi"][sb_]])
                k.op("dve", lambda e: e.tensor_tensor(out=a_lo[sb_][:], in0=a_sb[sb_][:], in1=a_hi[sb_][:], op=ALU.subtract),
                     reads=[R["a_sb"][sb_], R["a_hi"][sb_]], writes=[R["a_lo"][sb_]])

            def tri(g):
                s, j, b = g // 4, g % 4, g % 2
                sb_ = s % 2
                k.op("dve", lambda e: e.tensor_tensor(
                    out=TriH2[b][:], in0=self.tri_b[:, None, :].broadcast_to([128, 8, 128]),
                    in1=a_hi[sb_][:, j, :, None].broadcast_to([128, 8, 128]), op=ALU.mult),
                     reads=[R["a_hi"][sb_], self.r_const], writes=[R["TriH"][b]])
                k.op("pool", lambda e: e.tensor_tensor(
                    out=TriL2[b][:], in0=self.tri_b[:, None, :].broadcast_to([128, 8, 128]),
                    in1=a_lo[sb_][:, j, :, None].broadcast_to([128, 8, 128]), op=ALU.mult),
                     reads=[R["a_lo"][sb_], self.r_const], writes=[R["TriL"][b]])

            def front(g):
                s, j, b = g // 4, g % 4, g % 2
                sb_ = s % 2
                cs = slice(j * 128, (j + 1) * 128)
                xb = xbcs[sb_]
                r_xb = R["xbcs"][sb_]
                if g % 4 == 0:
                    tri(g)
                TriH, TriL = TriH2[b], TriL2[b]
                pT = ps[4][:, 0:384].bitcast(BF16)
                for c in range(6):
                    k.op("pe", lambda e, c=c: e.transpose(
                        out=pT[:, c * 128:(c + 1) * 128], in_=xb[:, c, cs], identity=self.ident_b[:]),
                         reads=[r_xb, self.r_const], writes=[r_ps[4]])
                for hf in range(2):
                    for (ti, TT, rn) in ((0, TriH, "TriH"), (1, TriL, "TriL")):
                        k.op("pe", lambda e, hf=hf, ti=ti, TT=TT: e.matmul(
                            ps[5 + hf][:, :], lhsT=self.ones_b[:, :],
                            rhs=TT[:, hf * 4:(hf + 1) * 4, :].rearrange("p e t -> p (e t)"), start=(ti == 0), stop=(ti == 1)),
                             reads=[R[rn][b], self.r_const], writes=[r_ps[5 + hf]])
                if (g + 1) % 4 != 0:
                    tri(g + 1)
                k.op("pe", lambda e: e.matmul(ps[7][:, 64:72], lhsT=self.tri_f, rhs=a_sb[sb_][:, j, :], start=True, stop=True),
                     reads=[R["a_sb"][sb_], self.r_const], writes=[r_ps[7]])
                for gq in range(2):
                    k.op("pe", lambda e, gq=gq: e.matmul(
                        ps[7][:, 128 + gq * 128:256 + gq * 128], lhsT=xb[:, 4 + gq, cs], rhs=xb[:, 6 + gq, cs],
                        start=True, stop=True), reads=[r_xb], writes=[r_ps[7]])
                k.op("act", lambda e: e.activation(out=xB[b][:], in_=pT, func=AF.Copy),
                     reads=[r_ps[4]], writes=[R["xB"][b]])
                k.op("dve", lambda e: e.tensor_copy(out=acs[:], in_=ps[7][:, 64:72]), reads=[r_ps[7]], writes=[R["acs"]])
                k.op("dve", lambda e: e.tensor_tensor(
                    out=CBm[:], in0=self.tri_b[:, None, :].broadcast_to([128, 2, 128]),
                    in1=ps[7][:, 128:384].rearrange("p (g t) -> p g t", g=2), op=ALU.mult),
                     reads=[r_ps[7], self.r_const], writes=[R["CBm"]])
                for hf in range(2):
                    al = ps[5 + hf][:, :].rearrange("p (e t) -> p e t", t=128)[:, :, 127]
                    k.op("dve", lambda e, hf=hf, al=al: e.tensor_tensor(
                        out=ndec[:, hf * 4:(hf + 1) * 4], in0=acs[:, hf * 4:(hf + 1) * 4], in1=al, op=ALU.subtract),
                         reads=[R["acs"], r_ps[5 + hf]], writes=[R["ndec"]])
                k.op("dve", lambda e: e.tensor_scalar(out=nacs[:], in0=acs[:], scalar1=-1.0, scalar2=None, op0=ALU.mult),
                     reads=[R["acs"]], writes=[R["nacs"]])
                for e_ in range(8):
                    hf = e_ // 4
                    k.op("act", lambda e, e_=e_, hf=hf: e.activation(
                        out=seg[:, e_, :], in_=ps[5 + hf][:, (e_ % 4) * 128:(e_ % 4 + 1) * 128],
                        func=AF.Identity, bias=nacs[:, e_:e_ + 1], scale=1.0),
                         reads=[r_ps[5 + hf], R["nacs"]], writes=[R["seg"]])
                k.op("dve", lambda e: e.tensor_scalar(out=seg[:], in0=seg[:], scalar1=0.0, scalar2=None, op0=ALU.min),
                     reads=[R["seg"]], writes=[R["seg"]])
                for hf in range(2):
                    al = ps[5 + hf][:, :].rearrange("p (e t) -> p e t", t=128)[:, :, 127]
                    k.op("act", lambda e, hf=hf, al=al: e.activation(out=cdec[b][:, hf * 4:(hf + 1) * 4], in_=al, func=AF.Exp),
                         reads=[r_ps[5 + hf]], writes=[R["cdec"][b]])
                    k.op("act", lambda e, hf=hf: e.activation(
                        out=Eacs[:, hf * 4:(hf + 1) * 4, :], in_=ps[5 + hf][:, :].rearrange("p (e t) -> p e t", t=128),
                        func=AF.Exp), reads=[r_ps[5 + hf]], writes=[R["Eacs"]])
                k.op("act", lambda e: e.activation(out=dec_s[:], in_=ndec[:], func=AF.Exp, scale=-1.0),
                     reads=[R["ndec"]], writes=[R["dec_s"]])
                k.op("act", lambda e: e.activation(out=LT[:], in_=seg[:], func=AF.Exp), reads=[R["seg"]], writes=[R["LT"]])

            def front_b(g):
                s, j, b = g // 4, g % 4, g % 2
                sb_ = s % 2
                cs = slice(j * 128, (j + 1) * 128)
                xb = xbcs[sb_]
                r_xb = R["xbcs"][sb_]
                for gq in range(2):
                    k.op("pool", lambda e, gq=gq: e.tensor_tensor(
                        out=CsT[b][:, gq * 4:(gq + 1) * 4, :], in0=Eacs[:, gq * 4:(gq + 1) * 4, :],
                        in1=xb[:, 6 + gq:7 + gq, cs].broadcast_to([128, 4, 128]), op=ALU.mult),
                         reads=[R["Eacs"], r_xb], writes=[R["CsT"][b]])
                k.op("pool", lambda e: e.tensor_tensor(
                    out=xdt[b][:], in0=xB[b][:, 0:512].rearrange("p (e q) -> p e q", q=64),
                    in1=dts[sb_][:, j, :, None].broadcast_to([128, 8, 64]), op=ALU.mult),
                     reads=[R["xB"][b], R["dts"][sb_]], writes=[R["xdt"][b]])
                k.op("pool", lambda e: e.tensor_tensor(
                    out=xdec[b][:], in0=xdt[b][:], in1=dec_s[:, :, None].broadcast_to([128, 8, 64]), op=ALU.mult),
                     reads=[R["xdt"][b], R["dec_s"]], writes=[R["xdec"][b]])
                for gq in range(2):
                    k.op("dve", lambda e, gq=gq: e.tensor_tensor(
                        out=MT[b][:, gq * 4:(gq + 1) * 4, :], in0=LT[:, gq * 4:(gq + 1) * 4, :],
                        in1=CBm[:, gq:gq + 1, :].broadcast_to([128, 4, 128]), op=ALU.mult),
                         reads=[R["LT"], R["CBm"]], writes=[R["MT"][b]])

            def back(g):
                s, j, b = g // 4, g % 4, g % 2
                sb_ = s % 2
                cs = slice(j * 128, (j + 1) * 128)
                xb = xbcs[sb_]
                r_xb = R["xbcs"][sb_]
                for e_ in range(8):
                    hp, pb = e_ // 2, e_ // 4
                    col = (e_ % 4) * 128
                    k.op("pe", lambda e, e_=e_, hp=hp, pb=pb, col=col: e.matmul(
                        ps[pb][:, col:col + 128], lhsT=xdt[b][:, 2 * hp:2 * hp + 2, :].rearrange("p e q -> p (e q)"),
                        rhs=MT[b][:, e_, :], start=True, stop=False),
                         reads=[R["xdt"][b], R["MT"][b]], writes=[r_ps[pb]])
                    k.op("pe", lambda e, e_=e_, hp=hp, pb=pb, col=col: e.matmul(
                        ps[pb][:, col:col + 128], lhsT=M.diagD[:, hp, :], rhs=xb[:, hp, cs], start=False, stop=False),
                         reads=[M.r_ssdc, r_xb], writes=[r_ps[pb]])
                    k.op("pe", lambda e, e_=e_, hp=hp, pb=pb, col=col: e.matmul(
                        ps[pb][:, col:col + 128], lhsT=M.S_bf[:, hp * 128:(hp + 1) * 128],
                        rhs=CsT[b][:, e_, :], start=False, stop=True),
                         reads=[M.r_Sbf, R["CsT"][b]], writes=[r_ps[pb]])
                for gq in range(2):
                    k.op("pe", lambda e, gq=gq: e.matmul(
                        ps[4][:, gq * 256:(gq + 1) * 256], lhsT=xB[b][:, 512 + gq * 128:512 + (gq + 1) * 128],
                        rhs=xdec[b][:, gq * 4:(gq + 1) * 4, :].rearrange("p e q -> p (e q)"), start=True, stop=True),
                         reads=[R["xB"][b], R["xdec"][b]], writes=[r_ps[4]])
                k.op("pool", lambda e: e.tensor_tensor(
                    out=tmpS[:, :].rearrange("p (e q) -> p e q", q=64), in0=M.S[:, :].rearrange("p (e q) -> p e q", q=64),
                    in1=cdec[b][:, :, None].broadcast_to([128, 8, 64]), op=ALU.mult),
                     reads=[R["cdec"][b], M.r_S], writes=[R["tmpS"]])
                k.op("dve", lambda e: e.tensor_tensor(out=M.S_bf[:], in0=tmpS[:], in1=ps[4][:, :], op=ALU.add),
                     reads=[R["tmpS"], r_ps[4]], writes=[M.r_Sbf])
                k.op("dve", lambda e: e.tensor_tensor(out=M.S[:], in0=tmpS[:], in1=ps[4][:, :], op=ALU.add),
                     reads=[R["tmpS"], r_ps[4]], writes=[M.r_S])
                for pb in range(2):
                    for par in range(2):
                        rows = slice(64 * par, 64 * par + 64)
                        src = ps[pb][rows, :].rearrange("p (h q t) -> p h q t", q=2, t=128)[:, :, par, :]
                        dst = ysb[sb_][rows, 2 * pb:2 * pb + 2, cs]
                        if par == 0:
                            k.op("act", lambda e, src=src, dst=dst: e.activation(out=dst, in_=src, func=AF.Copy),
                                 reads=[r_ps[pb]], writes=[R["ysb"][sb_]])
                        else:
                            k.op("dve", lambda e, src=src, dst=dst: e.tensor_copy(out=dst, in_=src),
                                 reads=[r_ps[pb]], writes=[R["ysb"][sb_]])

            def epilogue_a(s):
                sb_ = s % 2
                yb, r_yb = ysb[sb_], R["ysb"][sb_]
                k.op("dve", lambda e: e.tensor_tensor(out=yb[:], in0=yb[:], in1=zs[sb_][:], op=ALU.mult),
                     reads=[r_yb, R["zs"][sb_]], writes=[r_yb])
                k.op("act", lambda e: e.activation(out=ysq, in_=yb[:], func=AF.Square), reads=[r_yb], writes=[R["ysq"]])

            def epilogue_b(s):
                sb_ = s % 2
                sl = slice(s * 512, (s + 1) * 512)
                yb, r_yb = ysb[sb_], R["ysb"][sb_]
                for gq in range(2):
                    for c2 in range(2):
                        k.op("pe", lambda e, gq=gq, c2=c2: e.matmul(
                            ps[2 + gq][:, :], lhsT=self.ones_b[:, :], rhs=ysq[:, 2 * gq + c2, :], start=(c2 == 0), stop=(c2 == 1)),
                             reads=[R["ysq"], self.r_const], writes=[r_ps[2 + gq]])
                    k.op("act", lambda e, gq=gq: e.activation(out=rs[:, gq, :], in_=ps[2 + gq][:, :], func=AF.Sqrt,
                                                          bias=float(EPS), scale=1.0 / 256.0),
                         reads=[r_ps[2 + gq]], writes=[R["rs"]])
                k.op("dve", lambda e: e.reciprocal(out=rs, in_=rs), reads=[R["rs"]], writes=[R["rs"]])
                for c in range(4):
                    k.op("dve", lambda e, c=c: e.scalar_tensor_tensor(
                        out=M.y[:, 2 + c, sl], in0=yb[:, c, :], scalar=pk[:, po + 60 + c:po + 61 + c], in1=rs[:, c // 2, :],
                        op0=ALU.mult, op1=ALU.mult),
                         reads=[r_yb, R["rs"], self.r_const], writes=[M.r_y[2 + c]])

            for _ in prologue(0):
                pass
            pro = [None]
            pend = [None]

            def pro_step(n):
                for _ in range(n):
                    if pro[0] is None:
                        return
                    try:
                        next(pro[0])
                    except StopIteration:
                        pro[0] = None

            for g in range(16):
                s_ = g // 4
                if g % 4 == 0:
                    pro_step(1000)
                front(g)
                if pend[0] is not None:
                    epilogue_b(pend[0])
                    pend[0] = None
                    if g % 4 == 1 and s_ < 3:
                        pro[0] = prologue(s_ + 1)
                pro_step(2)
                if g > 0:
                    back(g - 1)
                    pro_step(2)
                front_b(g)
                if g > 0 and (g - 1) % 4 == 3:
                    epilogue_a((g - 1) // 4)
                    pend[0] = (g - 1) // 4
                if g == 0:
                    pro[0] = prologue(1)
                pro_step(3)
            back(15)
            epilogue_a(3)
            epilogue_b(3)
            k.op("pool", lambda e: e.tensor_copy(out=M.ctail[:, :, :], in_=cin[:, :, 0:3]), reads=[R["cin"]], writes=[M.r_ctail])
            k.barrier()

    def hgrn(self, t):
        k, M, l = self.k, self.M, self.M.l
        TM, po = M.TM, M.po
        ps, r_ps = M.ps, M.r_ps
        pk = self.pk
        with ExitStack() as es:
            f4 = lambda n: self.sb(es, n, [128, 2, 512], F32)
            b4 = lambda n: self.sb(es, n, [128, 2, 512], BF16)
            E, L1, LF, KK, Q, BC, TMP, EX = f4("hE"), f4("hL1"), f4("hLF"), f4("hKK"), f4("hQ"), f4("hBC"), f4("hTMP"), f4("hEX")
            G = [b4("hG0"), b4("hG1")]
            Qt = [b4("hQt0"), b4("hQt1")]
            Kt = [b4("hKt0"), b4("hKt1")]
            Qh = [b4("hQh0"), b4("hQh1")]
            Kh = [b4("hKh0"), b4("hKh1")]
            edec = [self.sb(es, f"hedec{i}", [128, 2, 8], F32) for i in range(2)]
            osb = f4("hosb")
            osq = b4("hosq")
            rs = f4("hrs")
            Vz = [self.sb(es, f"hVz{i}", [64, 2, 2, 128], BF16) for i in range(2)]
            AT = [self.sb(es, f"hAT{i}", [64, 2, 2, 64], BF16) for i in range(2)]
            KhT = [self.sb(es, f"hKhT{i}", [64, 2, 128], BF16) for i in range(2)]
            R = {n: k.res(n) for n in ("E", "L1", "LF", "KK", "Q", "BC", "TMP", "EX", "osb", "osq", "rs")}
            for n in ("G", "Qt", "Kt", "Qh", "Kh", "edec", "Vz", "AT", "KhT"):
                R[n] = [k.res(n + "0"), k.res(n + "1")]
            for i in range(2):
                k.op("pool", lambda e, i=i: e.memset(Vz[i][:], 0.0), writes=[R["Vz"][i]])
            whi, r_whi = self.mix_wload_n(self.win[l][:, :, 2824:2952], 128, 0)
            whi2, r_whi2 = self.mix_wload_n(self.win[l][:, :, 2952:3080], 128, 1)

            def prologue(s):
                sb_ = s % 2
                sl = slice(s * 512, (s + 1) * 512)
                first = True
                for (c0, kind) in ((2568, "f"), (2312, "q"), (3080, "g")):
                    w, r_w = self.mix_wload(self.win[l][:, :, c0:c0 + 256], 256)
                    for c in range(2):
                        if not first:
                            yield
                        first = False
                        pi = 2
                        for kc in range(8):
                            k.op("pe", lambda e, pi=pi, kc=kc, c=c, w=w: e.matmul(
                                ps[pi][:, :], lhsT=w[:, kc, c * 128:(c + 1) * 128], rhs=M.u[:, kc, sl],
                                start=(kc == 0), stop=(kc == 7)),
                                 reads=[r_w, M.r_u[s]], writes=[r_ps[pi]])
                        if kind == "q":
                            k.op("act", lambda e, pi=pi, c=c: e.activation(out=Q[:, c, :], in_=ps[pi][:, :], func=AF.Silu),
                                 reads=[r_ps[pi]], writes=[R["Q"]])
                        elif kind == "g":
                            k.op("act", lambda e, pi=pi, c=c: e.activation(out=G[sb_][:, c, :], in_=ps[pi][:, :], func=AF.Silu),
                                 reads=[r_ps[pi]], writes=[R["G"][sb_]])
                        else:
                            k.op("act", lambda e, pi=pi, c=c: e.activation(out=E[:, c, :], in_=ps[pi][:, :], func=AF.Exp, scale=-1.0),
                                 reads=[r_ps[pi]], writes=[R["E"]])
                yield
                for c in range(2):
                    k.op("act", lambda e, c=c: e.activation(out=L1[:, c, :], in_=E[:, c, :], func=AF.Ln, bias=1.0,
                                                          scale=M.lb[:, c:c + 1]),
                         reads=[R["E"], M.r_lb], writes=[R["L1"]])
                k.op("act", lambda e: e.activation(out=LF[:], in_=E[:], func=AF.Ln, bias=1.0, scale=1.0),
                     reads=[R["E"]], writes=[R["LF"]])
                k.op("pool", lambda e: e.tensor_tensor(out=LF[:], in0=L1[:], in1=LF[:], op=ALU.subtract),
                     reads=[R["L1"], R["LF"]], writes=[R["LF"]])
                yield
                k.op("act", lambda e: e.activation(out=KK[:], in_=LF[:], func=AF.Exp), reads=[R["LF"]], writes=[R["KK"]])
                k.op("pool", lambda e: e.tensor_scalar(out=KK[:], in0=KK[:], scalar1=-1.0, scalar2=1.0, op0=ALU.mult, op1=ALU.add),
                     reads=[R["KK"]], writes=[R["KK"]])
                for c in range(2):
                    k.op("dve", lambda e, c=c: e.tensor_tensor_scan(
                        out=BC[:, c, :], data0=self.rst_f, data1=LF[:, c, :], initial=0.0, op0=ALU.mult, op1=ALU.add),
                         reads=[R["LF"], self.r_const], writes=[R["BC"]])
                yield
                BC4 = BC[:, :, :].rearrange("p c (j t) -> p c j t", t=64)
                TMP4 = TMP[:, :, :].rearrange("p c (j t) -> p c j t", t=64)
                for c in range(2):
                    k.op("pool", lambda e, c=c: e.tensor_tensor(
                        out=TMP4[:, c], in0=BC4[:, c], in1=BC4[:, c, :, 31:32].broadcast_to([128, 8, 64]), op=ALU.subtract),
                         reads=[R["BC"]], writes=[R["TMP"]])
                k.op("act", lambda e: e.activation(out=EX[:], in_=TMP[:], func=AF.Exp), reads=[R["TMP"]], writes=[R["EX"]])
                k.op("dve", lambda e: e.tensor_tensor(out=Qt[sb_][:], in0=Q[:], in1=EX[:], op=ALU.mult),
                     reads=[R["Q"], R["EX"]], writes=[R["Qt"][sb_]])
                yield
                k.op("act", lambda e: e.activation(out=EX[:], in_=TMP[:], func=AF.Exp, scale=-1.0), reads=[R["TMP"]], writes=[R["EX"]])
                k.op("dve", lambda e: e.tensor_tensor(out=Kt[sb_][:], in0=KK[:], in1=EX[:], op=ALU.mult),
                     reads=[R["KK"], R["EX"]], writes=[R["Kt"][sb_]])
                yield
                k.op("act", lambda e: e.activation(out=EX[:], in_=BC[:], func=AF.Exp), reads=[R["BC"]], writes=[R["EX"]])
                k.op("dve", lambda e: e.tensor_tensor(out=Qh[sb_][:], in0=Q[:], in1=EX[:], op=ALU.mult),
                     reads=[R["Q"], R["EX"]], writes=[R["Qh"][sb_]])
                yield
                for c in range(2):
                    k.op("pool", lambda e, c=c: e.tensor_tensor(
                        out=TMP4[:, c], in0=BC4[:, c, :, 63:64].broadcast_to([128, 8, 64]), in1=BC4[:, c], op=ALU.subtract),
                         reads=[R["BC"]], writes=[R["TMP"]])
                    k.op("act", lambda e, c=c: e.activation(out=edec[sb_][:, c, :], in_=BC4[:, c, :, 63], func=AF.Exp),
                         reads=[R["BC"]], writes=[R["edec"][sb_]])
                k.op("act", lambda e: e.activation(out=EX[:], in_=TMP[:], func=AF.Exp), reads=[R["TMP"]], writes=[R["EX"]])
                k.op("dve", lambda e: e.tensor_tensor(out=Kh[sb_][:], in0=KK[:], in1=EX[:], op=ALU.mult),
                     reads=[R["KK"], R["EX"]], writes=[R["Kh"][sb_]])

            def stageA(g):
                s, j, b = g // 8, g % 8, g % 2
                sb_ = s % 2
                tk = slice(s * 512 + j * 64, s * 512 + (j + 1) * 64)
                cj = slice(j * 64, (j + 1) * 64)
                for (wv, r_wv, half) in ((whi, r_whi, 0), (whi2, r_whi2, 1)):
                    for kc in range(8):
                        k.op("pe", lambda e, kc=kc, wv=wv, half=half: e.matmul(
                            ps[4][0:64, half * 128:(half + 1) * 128], lhsT=M.u[:, kc, tk], rhs=wv[:, kc, 0:128],
                            start=(kc == 0), stop=(kc == 7)),
                             reads=[r_wv, M.r_u[s]], writes=[r_ps[4]])
                pK = ps[7][:, 0:128].bitcast(BF16)
                for c in range(2):
                    k.op("pe", lambda e, c=c: e.transpose(
                        out=pK[0:64, c * 128:(c + 1) * 128], in_=Kh[sb_][:, c, cj], identity=self.ident_b[:]),
                         reads=[R["Kh"][sb_], self.r_const], writes=[r_ps[7]])
                for c in range(2):
                    for par in range(2):
                        rows = slice(64 * par, 64 * par + 64)
                        k.op("pe", lambda e, c=c, par=par, rows=rows: e.matmul(
                            ps[5 + par][0:64, c * 64:(c + 1) * 64], lhsT=Kt[sb_][rows, c, cj], rhs=Qt[sb_][rows, c, cj],
                            start=True, stop=True),
                             reads=[R["Kt"][sb_], R["Qt"][sb_]], writes=[r_ps[5 + par]])
                for par in range(2):
                    k.op("dve", lambda e, par=par: e.tensor_copy(
                        out=Vz[b][:, :, par, par * 64:(par + 1) * 64],
                        in_=ps[4][0:64, 0:256].rearrange("s (c par v) -> s c par v", par=2, v=64)[:, :, par, :]),
                         reads=[r_ps[4]], writes=[R["Vz"][b]])
                k.op("act", lambda e: e.activation(
                    out=KhT[b][:, :, :], in_=pK[0:64, :].rearrange("s (c k) -> s c k", c=2), func=AF.Copy),
                     reads=[r_ps[7]], writes=[R["KhT"][b]])
                for par in range(2):
                    k.op("dve", lambda e, par=par: e.tensor_tensor(
                        out=AT[b][:, :, par, :], in0=self.tri_b[0:64, None, 0:64].broadcast_to([64, 2, 64]),
                        in1=ps[5 + par][0:64, 0:128].rearrange("s (c t) -> s c t", c=2), op=ALU.mult),
                         reads=[r_ps[5 + par], self.r_const], writes=[R["AT"][b]])

            def stageB(g):
                s, j, b = g // 8, g % 8, g % 2
                sb_ = s % 2
                cj = slice(j * 64, (j + 1) * 64)
                for c in range(2):
                    for par in range(2):
                        k.op("pe", lambda e, c=c, par=par: e.matmul(
                            ps[c][:, cj], lhsT=Vz[b][:, c, par, :], rhs=AT[b][:, c, par, :], start=(par == 0), stop=False),
                             reads=[R["Vz"][b], R["AT"][b]], writes=[r_ps[c]])
                    k.op("pe", lambda e, c=c: e.matmul(
                        ps[c][:, cj], lhsT=M.HS_bf[:, c, :], rhs=Qh[sb_][:, c, cj], start=False, stop=True),
                         reads=[M.r_HSb, R["Qh"][sb_]], writes=[r_ps[c]])
                for c in range(2):
                    for par in range(2):
                        k.op("pe", lambda e, c=c, par=par: e.matmul(
                            ps[3][:, c * 128:(c + 1) * 128], lhsT=KhT[b][:, c, :], rhs=Vz[b][:, c, par, :],
                            start=(par == 0), stop=(par == 1)),
                             reads=[R["KhT"][b], R["Vz"][b]], writes=[r_ps[3]])
                for c in range(2):
                    for par in range(2):
                        rows = slice(64 * par, 64 * par + 64)
                        for (dst, r_dst) in ((M.HS_bf, M.r_HSb), (M.HS, M.r_HS)):
                            k.op("dve", lambda e, c=c, par=par, rows=rows, dst=dst: e.scalar_tensor_tensor(
                                out=dst[rows, c, par * 64:(par + 1) * 64], in0=M.HS[rows, c, par * 64:(par + 1) * 64],
                                scalar=edec[sb_][rows, c, j:j + 1],
                                in1=ps[3][rows, c * 128 + par * 64:c * 128 + (par + 1) * 64],
                                op0=ALU.mult, op1=ALU.add),
                                 reads=[M.r_HS, R["edec"][sb_], r_ps[3]], writes=[r_dst])

            def epilogue(s):
                sb_ = s % 2
                sl = slice(s * 512, (s + 1) * 512)
                for c in range(2):
                    k.op("act", lambda e, c=c: e.activation(out=osb[:, c, :], in_=ps[c][:, :], func=AF.Copy),
                         reads=[r_ps[c]], writes=[R["osb"]])
                k.op("act", lambda e: e.activation(out=osq[:], in_=osb[:], func=AF.Square), reads=[R["osb"]], writes=[R["osq"]])
                for c in range(2):
                    k.op("pe", lambda e, c=c: e.matmul(ps[2 + c][:, :], lhsT=self.blk_b[:, :], rhs=osq[:, c, :], start=True, stop=True),
                         reads=[R["osq"], self.r_const], writes=[r_ps[2 + c]])
                    k.op("act", lambda e, c=c: e.activation(out=rs[:, c, :], in_=ps[2 + c][:, :], func=AF.Ln,
                                                          bias=float(EPS), scale=1.0 / 64.0),
                         reads=[r_ps[2 + c]], writes=[R["rs"]])
                k.op("act", lambda e: e.activation(out=rs[:], in_=rs[:], func=AF.Exp, scale=-0.5), reads=[R["rs"]], writes=[R["rs"]])
                k.op("pool", lambda e: e.tensor_tensor(out=rs[:], in0=rs[:], in1=G[sb_][:], op=ALU.mult),
                     reads=[R["rs"], R["G"][sb_]], writes=[R["rs"]])
                for c in range(2):
                    k.op("dve", lambda e, c=c: e.scalar_tensor_tensor(
                        out=M.y[:, 6 + c, sl], in0=osb[:, c, :], scalar=pk[:, po + 64 + c:po + 65 + c], in1=rs[:, c, :],
                        op0=ALU.mult, op1=ALU.mult),
                         reads=[R["osb"], R["rs"], self.r_const], writes=[M.r_y[6 + c]])

            for _ in prologue(0):
                pass
            pro = [None]

            def pro_step(n):
                for _ in range(n):
                    if pro[0] is None:
                        return
                    try:
                        next(pro[0])
                    except StopIteration:
                        pro[0] = None

            NG = 32
            for g in range(NG):
                s_ = g // 8
                if g % 8 == 0:
                    pro_step(1000)
                stageA(g)
                if g > 0:
                    stageB(g - 1)
                if g > 0 and (g - 1) % 8 == 7:
                    epilogue((g - 1) // 8)
                if g % 8 == 0 and s_ < 3:
                    pro[0] = prologue(s_ + 1)
                pro_step(2)
            stageB(NG - 1)
            epilogue(3)
            k.barrier()

    def attention(self, t):
        k, M, l = self.k, self.M, self.M.l
        TM = M.TM
        ps, r_ps = M.ps, M.r_ps
        with ExitStack() as es:
            X = [[self.sb(es, f"X{x}{p}", [128, TM], BF16) for p in range(3)] for x in range(3)]
            r_X = [[k.res(f"X{x}{p}") for p in range(3)] for x in range(3)]
            Vc = [self.sb(es, f"Vc{p}", [128, 16, 128], BF16) for p in range(3)]
            r_V = [k.res() for _ in range(3)]
            OD = self.sb(es, "OD", [128, 2, TM], F32)
            Oacc = OD[:, 0, :]
            den = OD[:, 1, :]
            r_acc = [[k.res("Oacc0"), k.res("Oacc1")], [k.res("den0"), k.res("den1")]]
            PT = [self.sb(es, f"PT{i}", [128, 512], BF16) for i in range(3)]
            r_PT = [k.res() for _ in range(3)]
            blk_tok = [
                lambda b: slice(128 * b, 128 * b + 128),
                lambda b: slice(512 * (b // 4) + (b % 4), 512 * (b // 4) + 512, 4),
                lambda b: slice(b, TM, 16),
            ]

            def dst_ap(buf, p, s):
                if p == 0:
                    return buf[:, 512 * s:512 * s + 512]
                if p == 1:
                    return buf[:, 512 * s:512 * s + 512].rearrange("p (r i) -> p i r", r=4)
                return buf[:, :].rearrange("p (r i) -> p i r", r=16)[:, 32 * s:32 * s + 32, :]

            def src_ap(psb, p):
                if p == 0:
                    return psb[:, :]
                return psb[:, :].rearrange("p (i r) -> p i r", r=(4 if p == 1 else 16))

            cnt = 0
            ev = 0
            for hp in range(2):
                ws = []
                for x in range(3):
                    ws.append(self.mix_wload_n(self.win[l][:, :, 256 * x + hp * 128:256 * x + hp * 128 + 128], 128, x))
                for s in range(4):
                    sl = slice(s * 512, (s + 1) * 512)
                    for x in range(3):
                        ww, r_ww = ws[x]
                        pi = 1 + (s * 3 + x) % 3
                        for kc in range(8):
                            k.op("pe", lambda e, pi=pi, kc=kc, ww=ww, sl=sl: e.matmul(
                                ps[pi][:, :], lhsT=ww[:, kc, 0:128], rhs=M.u[:, kc, sl], start=(kc == 0), stop=(kc == 7)),
                                 reads=[r_ww, M.r_u[s]], writes=[r_ps[pi]])
                        for p in range(3):
                            ev += 1
                            if ev % 2 == 0:
                                k.op("act", lambda e, pi=pi, x=x, p=p, s=s: e.activation(
                                    out=dst_ap(X[x][p], p, s), in_=src_ap(ps[pi], p), func=AF.Copy),
                                     reads=[r_ps[pi]], writes=[r_X[x][p]])
                            else:
                                k.op("dve", lambda e, pi=pi, x=x, p=p, s=s: e.tensor_copy(
                                    out=dst_ap(X[x][p], p, s), in_=src_ap(ps[pi], p)),
                                     reads=[r_ps[pi]], writes=[r_X[x][p]])
                import os
                ADBG = int(os.environ.get("ADBG", "9"))
                for p in range(3):
                    if ADBG < 1:
                        break
                    for b4 in range(4):
                        pi = 4 + (p * 4 + b4) % 2
                        pv = ps[pi][:, 0:256].bitcast(BF16)
                        for bb in range(4):
                            b = b4 * 4 + bb
                            k.op("pe", lambda e, pv=pv, bb=bb, b=b, p=p: e.transpose(
                                out=pv[:, bb * 128:(bb + 1) * 128], in_=X[2][p][:, 128 * b:128 * b + 128],
                                identity=self.ident_b[:]),
                                 reads=[r_X[2][p], self.r_const], writes=[r_ps[pi]])
                        k.op("dve", lambda e, pv=pv, p=p, b4=b4: e.tensor_copy(
                            out=Vc[p][:, b4 * 4:(b4 + 1) * 4, :], in_=pv.rearrange("p (b c) -> p b c", b=4)),
                             reads=[r_ps[pi]], writes=[r_V[p]])
                blocks = [(p, b) for p in range(3) for b in range(16)]
                info = {}

                def stageA(i):
                    p, b = blocks[i]
                    tk = blk_tok[p](b)
                    bs = slice(128 * b, 128 * b + 128)
                    pb = b - (1, 4, 16)[p]
                    if pb >= 0:
                        prev = (X[1][p], slice(128 * pb, 128 * pb + 128), Vc[p][:, pb, :], [r_X[1][p], r_V[p]])
                    elif t > 0:
                        kp = (M.kTp1, M.kTp2, M.kTp3)[p]
                        vp = (M.V1p[:, hp, :], M.V2p[:, hp, b % 4, :], M.V3p[:, hp, b, :])[p]
                        off = (0, 128 * (b % 4), 128 * b)[p]
                        prev = (kp[:, hp, :], slice(off, off + 128), vp, [M.r_prev[hp]])
                    else:
                        prev = None
                    cur = (X[1][p], bs, Vc[p][:, b, :], [r_X[1][p], r_V[p]])
                    sp_ = prev if prev is not None else cur
                    pSs = (2 + i % 3, 5 + i % 3)
                    pt, r_pt = PT[i % 3], r_PT[i % 3]
                    info[i] = (p, tk, prev, cur, pt, r_pt)
                    mo = 0 if prev is not None else 512
                    for par in range(2):
                        pS = pSs[par]
                        k.op("pe", lambda e, pS=pS, mo=mo: e.matmul(
                            ps[pS][:, 0:256], lhsT=self.ident_b[:, :], rhs=self.mbias_b[:, mo:mo + 256], start=True, stop=False),
                             reads=[self.r_const], writes=[r_ps[pS]])
                    for par in range(2):
                        rows = slice(64 * par, 64 * par + 64)
                        pS = pSs[par]
                        for j, kb in enumerate((sp_, cur)):
                            k.op("pe", lambda e, pS=pS, j=j, kb=kb, rows=rows, bs=bs, p=p: e.matmul(
                                ps[pS][:, j * 128:(j + 1) * 128],
                                lhsT=kb[0][rows, kb[1]], rhs=X[0][p][rows, bs], start=False, stop=(j == 1)),
                                 reads=kb[3] + [r_X[0][p]], writes=[r_ps[pS]])
                    for par in range(2):
                        pS = pSs[par]
                        k.op("act", lambda e, pS=pS, pt=pt, par=par: e.activation(
                            out=pt[:, par * 256:(par + 1) * 256], in_=ps[pS][:, 0:256], func=AF.Exp, scale=0.125),
                             reads=[r_ps[pS]], writes=[r_pt])

                def stageB(i):
                    p, tk, prev, cur, pt, r_pt = info.pop(i)
                    pO = i % 2
                    kbs = ([prev] if prev is not None else []) + [cur]
                    for par in range(2):
                        for (col, lh) in ((par * 128, None), (256 + par * 128, self.ones_b)):
                            for j, kb in enumerate(kbs):
                                jj = j if prev is not None else 1
                                lhsT = kb[2] if lh is None else lh[:, :]
                                k.op("pe", lambda e, pO=pO, col=col, lhsT=lhsT, pt=pt, par=par, jj=jj, j=j, n=len(kbs): e.matmul(
                                    ps[pO][:, col:col + 128], lhsT=lhsT, rhs=pt[:, par * 256 + jj * 128:par * 256 + (jj + 1) * 128],
                                    start=(j == 0), stop=(j == n - 1)),
                                     reads=kb[3] + [r_pt, self.r_const], writes=[r_ps[pO]])
                    for par in range(2):
                        rows = slice(64 * par, 64 * par + 64)
                        r_dst = r_acc[0][par]
                        src = ps[pO][rows, :].rearrange("p (q h t) -> p q h t", q=2, h=2)[:, :, par, :]
                        if p == 0:
                            k.op("dve", lambda e, rows=rows, tk=tk, src=src: e.tensor_copy(out=OD[rows, :, tk], in_=src),
                                 reads=[r_ps[pO]], writes=[r_dst])
                        else:
                            k.op("dve", lambda e, rows=rows, tk=tk, src=src: e.tensor_tensor(
                                out=OD[rows, :, tk], in0=OD[rows, :, tk], in1=src, op=ALU.add),
                                 reads=[r_ps[pO], r_dst], writes=[r_dst])

                nblk = len(blocks)
                stageA(0)
                stageA(1)
                for i in range(nblk):
                    if i + 2 < nblk:
                        stageA(i + 2)
                    stageB(i)
                r_all = r_acc[0]
                k.op("act", lambda e: e.activation(out=den, in_=den, func=AF.Ln), reads=r_all, writes=r_all)
                k.op("act", lambda e: e.activation(out=den, in_=den, func=AF.Exp, scale=-1.0), reads=r_all, writes=r_all)
                k.op("dve", lambda e, hp=hp: e.tensor_tensor(out=M.y[:, hp, :], in0=Oacc, in1=den, op=ALU.mult),
                     reads=r_all, writes=[M.r_y[hp]])
                k.op("pool", lambda e, hp=hp: e.tensor_copy(out=M.kTp1[:, hp, :], in_=X[1][0][:, TM - 128:TM]),
                     reads=[r_X[1][0]], writes=[M.r_prev[hp]])
                k.op("pool", lambda e, hp=hp: e.tensor_copy(out=M.kTp2[:, hp, :], in_=X[1][1][:, TM - 512:TM]),
                     reads=[r_X[1][1]], writes=[M.r_prev[hp]])
                k.op("pool", lambda e, hp=hp: e.tensor_copy(out=M.kTp3[:, hp, :], in_=X[1][2][:, :]),
                     reads=[r_X[1][2]], writes=[M.r_prev[hp]])
                k.op("pool", lambda e, hp=hp: e.tensor_copy(out=M.V1p[:, hp, :], in_=Vc[0][:, 15, :]),
                     reads=[r_V[0]], writes=[M.r_prev[hp]])
                k.op("pool", lambda e, hp=hp: e.tensor_copy(out=M.V2p[:, hp, :, :], in_=Vc[1][:, 12:16, :]),
                     reads=[r_V[1]], writes=[M.r_prev[hp]])
                k.op("pool", lambda e, hp=hp: e.tensor_copy(out=M.V3p[:, hp, :, :], in_=Vc[2][:, :, :]),
                     reads=[r_V[2]], writes=[M.r_prev[hp]])
            k.barrier()


def make_consts():
    i = np.arange(128)
    ident = np.eye(128, dtype=np.float32)
    prev = (i[:, None] >= i[None, :]).astype(np.float32)
    cur = (i[:, None] <= i[None, :]).astype(np.float32)
    z = np.zeros((128, 128), np.float32)
    blk = (i[:, None] // 64 == i[None, :] // 64).astype(np.float32)
    rst = np.broadcast_to((np.arange(512) % 64 != 0).astype(np.float32)[None, :], (128, 512))
    return np.ascontiguousarray(np.concatenate([ident, prev, cur, prev, cur, z, cur, z, cur, blk, rst], axis=1))


PK_L = 72


def make_packed(inputs):
    g = lambda n: np.asarray(inputs[n], dtype=np.float32)
    pidx = np.arange(128)
    cols = []
    for l in range(DEPTH):
        cw = g("conv_w")[l]
        cols.append(cw.reshape(4, 8, 128).transpose(2, 0, 1).reshape(128, 32))
        cols.append(g("conv_b")[l].reshape(8, 128).T)
        cols.append(np.broadcast_to(g("a_log")[l][None, :], (128, 8)))
        cols.append(np.broadcast_to(g("dt_bias")[l][None, :], (128, 8)))
        hd = (2 * np.arange(4)[None, :] + (pidx[:, None] // 64))
        cols.append(g("d_skip")[l][hd])
        cols.append(g("ssm_norm")[l].reshape(4, 128).T)
        cols.append(g("hgrn_norm")[l].reshape(2, 128).T)
        cols.append(g("hgrn_lb_logits")[0].reshape(2, 128).T)
        cols.append(g("hgrn_lb_logits")[1].reshape(2, 128).T)
        cols.append(np.zeros((128, PK_L - 70), np.float32))
    return np.ascontiguousarray(np.concatenate(cols, axis=1), dtype=np.float32)


def make_in_map(inputs, b, L=None):
    m = {}
    for n, v in inputs.items():
        v = np.asarray(v)
        if n == "x":
            xs = v[b]
            if L is not None:
                xs = xs[:L]
            m["x"] = np.ascontiguousarray(xs, dtype=np.float32)
        else:
            m[n] = np.ascontiguousarray(v, dtype=np.float32)
    m["cst_d"] = make_consts()
    m["pk_d"] = make_packed(inputs)
    return m


def build_two_pass(L, **kw):
    p1 = Prog(L, **kw)
    p1.build()
    needed = p1.k.needed
    del p1
    p2 = Prog(L, needed_in=needed, **kw)
    p2.build()
    return p2


def kernel(**inputs):
    x = np.asarray(inputs["x"])
    B, L, _ = x.shape
    prog = build_two_pass(L)
    nc = prog.nc
    in_maps = [make_in_map(inputs, b) for b in range(B)]
    res = run_bass_kernel_spmd(nc, in_maps, core_ids=list(range(B)))
    return np.stack([np.asarray(r["out"], dtype=np.float32) for r in res.results], axis=0)
```
